# Optimizing a Trainium2 kernel written in Bass

```python
import math
import jax
import jax.numpy as jnp
from jax import lax
import numpy as np

D_MODEL = 1024
BATCH = 8
SEQ = 4096
DEPTH = 4

GRID_W = 64
CTX_LEN = 256
N_MIXERS = 3
Q_BLOCK = 128
ROPE_THETA = 10000.0
NORM_EPS = 1e-6
N_MOD = 9
D_FF = 2816

DA_HEADS = 8
DA_HEAD_DIM = D_MODEL // (2 * DA_HEADS)
GQ_HEADS = 8
GQ_KV_HEADS = 2
GQ_HEAD_DIM = D_MODEL // GQ_HEADS
GQ_GROUP = GQ_HEADS // GQ_KV_HEADS
GQ_QKV = (GQ_HEADS + 2 * GQ_KV_HEADS) * GQ_HEAD_DIM
RW_HEAD = 64
RW_HEADS = D_MODEL // RW_HEAD
RW_DECAY_LORA = 64
RW_AAA_LORA = 64
RW_GATE_LORA = 128
RW_GN_EPS = 64e-5

kernel_name = "hybrid_dit_diffattn_gqa_rwkv7_macaron"


def rms_norm(x, gain=None):
    xf = x.astype(jnp.float32)
    y = xf * lax.rsqrt(jnp.mean(xf * xf, axis=-1, keepdims=True) + NORM_EPS)
    if gain is not None:
        y = y * gain.astype(jnp.float32)
    return y.astype(x.dtype)


def modulate(h, shift, scale):
    return rms_norm(h) * (1 + scale) + shift


def swiglu(x, wg, wu, wd):
    return (jax.nn.silu(x @ wg) * (x @ wu)) @ wd


def axial_rope(n_tokens, head_dim, dtype):
    rows = n_tokens // GRID_W
    row = jnp.repeat(jnp.arange(rows, dtype=jnp.int32), GRID_W).astype(jnp.float32)
    col = jnp.tile(jnp.arange(GRID_W, dtype=jnp.int32), rows).astype(jnp.float32)
    n_freq = head_dim // 4
    inv = jnp.power(ROPE_THETA, -jnp.arange(n_freq, dtype=jnp.float32) / n_freq)
    ang = jnp.concatenate([row[:, None] * inv, col[:, None] * inv], axis=-1)
    return jnp.cos(ang).astype(dtype), jnp.sin(ang).astype(dtype)


def apply_rope(x, cos, sin):
    shape = (cos.shape[0],) + (1,) * (x.ndim - 3) + (cos.shape[-1],)
    cos = cos.reshape(shape)
    sin = sin.reshape(shape)
    x1, x2 = jnp.split(x, 2, axis=-1)
    return jnp.concatenate([x1 * cos - x2 * sin, x2 * cos + x1 * sin], axis=-1)


def sweep_query_blocks(fn, q):
    b, s = q.shape[:2]
    nb = s // Q_BLOCK
    qb = jnp.moveaxis(q.reshape((b, nb, Q_BLOCK) + q.shape[2:]), 1, 0)
    out = jnp.moveaxis(lax.map(fn, qb), 0, 1)
    return out.reshape((b, s) + out.shape[3:])


def diff_attn_core(q, k, v, lam):
    s = jnp.einsum('bqhcd,bkhcd->bhcqk', q, k, preferred_element_type=jnp.float32) * (DA_HEAD_DIM ** -0.5)
    p = jax.nn.softmax(s, axis=-1)
    a = p[:, :, 0] - lam * p[:, :, 1]
    return jnp.einsum('bhqk,bkhe->bqhe', a.astype(v.dtype), v)


def differential_attention(u, uc, wqkv, wo, lam_vecs, subln_g, lam_init, need_ctx):
    b, s, _ = u.shape

    def project(z):
        t = z.shape[1]
        q, k, v = jnp.split(z @ wqkv, 3, axis=-1)
        return (q.reshape(b, t, DA_HEADS, 2, DA_HEAD_DIM),
                k.reshape(b, t, DA_HEADS, 2, DA_HEAD_DIM),
                v.reshape(b, t, DA_HEADS, 2 * DA_HEAD_DIM))

    q, k, v = project(u)
    qc, kc, vc = project(uc)
    cos, sin = axial_rope(s, DA_HEAD_DIM, u.dtype)
    q = apply_rope(q, cos, sin)
    k = apply_rope(k, cos, sin)
    lv = lam_vecs.astype(jnp.float32)
    lam = jnp.exp(jnp.sum(lv[0] * lv[1])) - jnp.exp(jnp.sum(lv[2] * lv[3])) + lam_init
    k_all = jnp.concatenate([kc, k], axis=1)
    v_all = jnp.concatenate([vc, v], axis=1)

    def finish(o):
        o = rms_norm(o, subln_g) * (1.0 - lam_init)
        return o.reshape(o.shape[0], o.shape[1], D_MODEL) @ wo

    y = finish(sweep_query_blocks(lambda qb: diff_attn_core(qb, k_all, v_all, lam), q))
    yc = finish(diff_attn_core(qc, kc, vc, lam)) if need_ctx else None
    return y, yc


def gqa_core(q, k, v):
    s = jnp.einsum('bqgmd,bkgd->bgmqk', q, k, preferred_element_type=jnp.float32) * (GQ_HEAD_DIM ** -0.5)
    p = jax.nn.softmax(s, axis=-1)
    return jnp.einsum('bgmqk,bkgd->bqgmd', p.astype(v.dtype), v)


def grouped_query_attention(u, uc, wqkv, wo, qk_g, need_ctx):
    b, s, _ = u.shape
    nq = GQ_HEADS * GQ_HEAD_DIM
    nk = GQ_KV_HEADS * GQ_HEAD_DIM

    def project(z):
        t = z.shape[1]
        qkv = z @ wqkv
        q = qkv[..., :nq].reshape(b, t, GQ_KV_HEADS, GQ_GROUP, GQ_HEAD_DIM)
        k = qkv[..., nq:nq + nk].reshape(b, t, GQ_KV_HEADS, GQ_HEAD_DIM)
        v = qkv[..., nq + nk:].reshape(b, t, GQ_KV_HEADS, GQ_HEAD_DIM)
        return rms_norm(q, qk_g[0]), rms_norm(k, qk_g[1]), v

    q, k, v = project(u)
    qc, kc, vc = project(uc)
    cos, sin = axial_rope(s, GQ_HEAD_DIM, u.dtype)
    q = apply_rope(q, cos, sin)
    k = apply_rope(k, cos, sin)
    k_all = jnp.concatenate([kc, k], axis=1)
    v_all = jnp.concatenate([vc, v], axis=1)

    def finish(o):
        return o.reshape(o.shape[0], o.shape[1], D_MODEL) @ wo

    y = finish(sweep_query_blocks(lambda qb: gqa_core(qb, k_all, v_all), q))
    yc = finish(gqa_core(qc, kc, vc)) if need_ctx else None
    return y, yc


def rwkv_token_inputs(z, mu, wr, wk, wv, w0, w1, w2, a0, a1, a2, g1, g2, k_k, k_a, r_k):
    b, t, _ = z.shape
    zero = jnp.zeros_like(z[:, :1])
    d_prev = jnp.concatenate([zero, z[:, :-1]], axis=1) - z
    d_next = jnp.concatenate([z[:, 1:], zero], axis=1) - z

    def mix(i):
        return z + d_prev * mu[0, i] + d_next * mu[1, i]

    def heads(y):
        return y.reshape(b, t, RW_HEADS, RW_HEAD)

    xw, xa, xg = mix(1), mix(4), mix(5)
    r = heads(mix(0) @ wr)
    k = mix(2) @ wk
    v = heads(mix(3) @ wv)
    dirs = []
    for d in range(2):
        w_log = -jax.nn.softplus(-(w0[d] + jnp.tanh(xw @ w1[d]) @ w2[d])) - 0.5
        decay = jnp.exp(-jnp.exp(w_log.astype(jnp.float32)))
        a = jax.nn.sigmoid(a0[d] + (xa @ a1[d]) @ a2[d])
        g = jax.nn.sigmoid(xg @ g1[d]) @ g2[d]
        kk = heads(k * k_k[d]).astype(jnp.float32)
        kk = kk / jnp.maximum(jnp.linalg.norm(kk, axis=-1, keepdims=True), 1e-12)
        kd = heads(k * (1 + (a - 1) * k_a[d]))
        bonus = jnp.sum((r * kd * r_k[d]).astype(jnp.float32), axis=-1, keepdims=True) * v.astype(jnp.float32)
        dirs.append((heads(decay), kd, -kk, kk * heads(a).astype(jnp.float32), g, bonus))
    return r, v, dirs


def wkv7_scan(state, r, decay, k, v, a, b, reverse):
    xs = tuple(jnp.moveaxis(z.astype(jnp.float32), 1, 0) for z in (r, decay, k, v, a, b))

    def step(S, inp):
        r_t, w_t, k_t, v_t, a_t, b_t = inp
        sa = jnp.einsum('bhvk,bhk->bhv', S, a_t)
        S = S * w_t[:, :, None, :] + sa[..., None] * b_t[:, :, None, :] + v_t[..., None] * k_t[:, :, None, :]
        return S, jnp.einsum('bhvk,bhk->bhv', S, r_t)

    state, ys = lax.scan(step, state, xs, reverse=reverse)
    return state, jnp.moveaxis(ys, 0, 1)


def rwkv_readout(y, bonus, g, ln_g, ln_b, dtype):
    b, t = y.shape[:2]
    mean = jnp.mean(y, axis=-1, keepdims=True)
    var = jnp.mean(jnp.square(y - mean), axis=-1, keepdims=True)
    yn = ((y - mean) * lax.rsqrt(var + RW_GN_EPS)).reshape(b, t, D_MODEL)
    out = yn * ln_g.astype(jnp.float32) + ln_b.astype(jnp.float32) + bonus.reshape(b, t, D_MODEL)
    return out.astype(dtype) * g


def rwkv7_bidirectional(u, uc, mu, wr, wk, wv, wo, w0, w1, w2, a0, a1, a2, g1, g2, k_k, k_a, r_k,
                        ln_g, ln_b, need_ctx):
    params = (mu, wr, wk, wv, w0, w1, w2, a0, a1, a2, g1, g2, k_k, k_a, r_k)
    r, v, dirs = rwkv_token_inputs(u, *params)
    rc, vc, dirs_c = rwkv_token_inputs(uc, *params)
    state0 = jnp.zeros((u.shape[0], RW_HEADS, RW_HEAD, RW_HEAD), jnp.float32)
    outs, outs_c = [], []
    for d in range(2):
        reverse = d == 1
        dc, kdc, ac, bc, gc, bonc = dirs_c[d]
        dl, kdl, al, bl, gl, bonl = dirs[d]
        state_c, wkv_c = wkv7_scan(state0, rc, dc, kdc, vc, ac, bc, reverse)
        _, wkv_l = wkv7_scan(state_c, r, dl, kdl, v, al, bl, reverse)
        outs.append(rwkv_readout(wkv_l, bonl, gl, ln_g[d], ln_b[d], u.dtype))
        if need_ctx:
            outs_c.append(rwkv_readout(wkv_c, bonc, gc, ln_g[d], ln_b[d], u.dtype))
    y = (outs[0] + outs[1]) @ wo
    yc = (outs_c[0] + outs_c[1]) @ wo if need_ctx else None
    return y, yc


def setup_inputs(seed: int = 0) -> dict:
    key = jax.random.key(seed)
    ks = iter(jax.random.split(key, 48))
    D = D_MODEL
    n_a = (DEPTH + 2) // N_MIXERS
    n_b = (DEPTH + 1) // N_MIXERS
    n_c = DEPTH // N_MIXERS

    def nrm(shape, scale):
        return jax.random.normal(next(ks), shape, jnp.float32) * scale

    def uni(shape, lo, hi):
        return jax.random.uniform(next(ks), shape, jnp.float32, lo, hi)

    return {
        "x": nrm((BATCH, SEQ, D), 1.0),
        "c": nrm((BATCH, D), 1.0),
        "ctx": nrm((BATCH, CTX_LEN, D), 1.0),
        "c_ctx": nrm((D,), 1.0),
        "ada_w": nrm((DEPTH, D, N_MOD * D), 0.5 * D ** -0.5),
        "ada_b": nrm((DEPTH, N_MOD * D), 0.02),
        "ffn_wg": nrm((DEPTH, 2, D, D_FF), D ** -0.5),
        "ffn_wu": nrm((DEPTH, 2, D, D_FF), D ** -0.5),
        "ffn_wd": nrm((DEPTH, 2, D_FF, D), D_FF ** -0.5),
        "final_g": 1.0 + nrm((D,), 0.02),
        "da_wqkv": nrm((n_a, D, 3 * D), D ** -0.5),
        "da_wo": nrm((n_a, D, D), D ** -0.5),
        "da_lam": nrm((n_a, 4, DA_HEAD_DIM), 0.1),
        "da_subln": 1.0 + nrm((n_a, 2 * DA_HEAD_DIM), 0.02),
        "gq_wqkv": nrm((n_b, D, GQ_QKV), D ** -0.5),
        "gq_wo": nrm((n_b, GQ_HEADS * GQ_HEAD_DIM, D), D ** -0.5),
        "gq_qk_g": 1.0 + nrm((n_b, 2, GQ_HEAD_DIM), 0.02),
        "rw_mu": uni((n_c, 2, 6, D), 0.0, 0.5),
        "rw_wr": nrm((n_c, D, D), D ** -0.5),
        "rw_wk": nrm((n_c, D, D), D ** -0.5),
        "rw_wv": nrm((n_c, D, D), D ** -0.5),
        "rw_wo": nrm((n_c, D, D), D ** -0.5),
        "rw_w0": uni((n_c, 2, D), -6.0, -1.0),
        "rw_w1": nrm((n_c, 2, D, RW_DECAY_LORA), D ** -0.5),
        "rw_w2": nrm((n_c, 2, RW_DECAY_LORA, D), 0.1),
        "rw_a0": nrm((n_c, 2, D), 0.1),
        "rw_a1": nrm((n_c, 2, D, RW_AAA_LORA), D ** -0.5),
        "rw_a2": nrm((n_c, 2, RW_AAA_LORA, D), 0.1),
        "rw_g1": nrm((n_c, 2, D, RW_GATE_LORA), D ** -0.5),
        "rw_g2": nrm((n_c, 2, RW_GATE_LORA, D), RW_GATE_LORA ** -0.5),
        "rw_kk": 0.85 + nrm((n_c, 2, D), 0.02),
        "rw_ka": 1.0 + nrm((n_c, 2, D), 0.02),
        "rw_rk": nrm((n_c, 2, RW_HEADS, RW_HEAD), 0.1),
        "rw_ln_g": 1.0 + nrm((n_c, 2, D), 0.02),
        "rw_ln_b": nrm((n_c, 2, D), 0.02),
    }


def reference(x, c, ctx, c_ctx, ada_w, ada_b, ffn_wg, ffn_wu, ffn_wd, final_g,
              da_wqkv, da_wo, da_lam, da_subln, gq_wqkv, gq_wo, gq_qk_g,
              rw_mu, rw_wr, rw_wk, rw_wv, rw_wo, rw_w0, rw_w1, rw_w2, rw_a0, rw_a1, rw_a2,
              rw_g1, rw_g2, rw_kk, rw_ka, rw_rk, rw_ln_g, rw_ln_b):
    h, hc = x, ctx
    c_act = jax.nn.silu(c)
    cc_act = jax.nn.silu(c_ctx)
    for i in range(DEPTH):
        kind = i % N_MIXERS
        j = i // N_MIXERS
        need_ctx = i < DEPTH - 1
        mod = jnp.split((c_act @ ada_w[i] + ada_b[i])[:, None, :], N_MOD, axis=-1)
        modc = jnp.split(cc_act @ ada_w[i] + ada_b[i], N_MOD, axis=-1)
        h = h + 0.5 * mod[2] * swiglu(modulate(h, mod[0], mod[1]), ffn_wg[i, 0], ffn_wu[i, 0], ffn_wd[i, 0])
        hc = hc + 0.5 * modc[2] * swiglu(modulate(hc, modc[0], modc[1]), ffn_wg[i, 0], ffn_wu[i, 0], ffn_wd[i, 0])
        u = modulate(h, mod[3], mod[4])
        uc = modulate(hc, modc[3], modc[4])
        if kind == 0:
            lam_init = 0.8 - 0.6 * math.exp(-0.3 * i)
            y, yc = differential_attention(u, uc, da_wqkv[j], da_wo[j], da_lam[j], da_subln[j], lam_init, need_ctx)
        elif kind == 1:
            y, yc = grouped_query_attention(u, uc, gq_wqkv[j], gq_wo[j], gq_qk_g[j], need_ctx)
        else:
            y, yc = rwkv7_bidirectional(u, uc, rw_mu[j], rw_wr[j], rw_wk[j], rw_wv[j], rw_wo[j],
                                        rw_w0[j], rw_w1[j], rw_w2[j], rw_a0[j], rw_a1[j], rw_a2[j],
                                        rw_g1[j], rw_g2[j], rw_kk[j], rw_ka[j], rw_rk[j],
                                        rw_ln_g[j], rw_ln_b[j], need_ctx)
        h = h + mod[5] * y
        h = h + 0.5 * mod[8] * swiglu(modulate(h, mod[6], mod[7]), ffn_wg[i, 1], ffn_wu[i, 1], ffn_wd[i, 1])
        if need_ctx:
            hc = hc + modc[5] * yc
            hc = hc + 0.5 * modc[8] * swiglu(modulate(hc, modc[6], modc[7]), ffn_wg[i, 1], ffn_wu[i, 1], ffn_wd[i, 1])
    return rms_norm(h, final_g)
```

```python
import math
from contextlib import ExitStack
import numpy as np
import ml_dtypes
import concourse.bass as bass
import concourse.mybir as mybir
from concourse.bass_utils import run_bass_kernel_spmd

F32 = mybir.dt.float32
BF16 = mybir.dt.bfloat16
AF = mybir.ActivationFunctionType
ALU = mybir.AluOpType
AX = mybir.AxisListType

ENGS = ('pe', 'dve', 'act', 'pool', 'sp')
DMAQ = ('sp', 'pool', 'act')

D = 1024
DFF = 2816
NCTX = 256
SEQ = 4096
T = NCTX + SEQ
DEPTH = 4
EPS = 1e-6
TILES = [(0, 256)] + [(256 + 512 * i, 512) for i in range(8)]


_UID = [0]


class Dep:
    __slots__ = ('w', 'r')

    def __init__(self):
        self.w = {}
        self.r = {}


class _Rec:
    def __getattr__(self, name):
        def f(*a, **k):
            self.call = (name, a, k)
            return self
        return f


class Sched:
    def __init__(self, nc, ring=6):
        self.nc = nc
        self.q = {e: [] for e in ENGS}
        self.semobj = {}
        self.cnt = {e: 0 for e in ENGS}
        self.waited = {e: {} for e in ENGS}
        for e in ENGS:
            self.semobj['c_' + e] = nc.alloc_semaphore('c_' + e)
        self.K = ring
        self.dcount = {e: 0 for e in DMAQ}
        self.ringlast = {}
        for e in DMAQ:
            for i in range(ring):
                sid = f'd_{e}_{i}'
                self.semobj[sid] = nc.alloc_semaphore(sid)
                self.ringlast[sid] = 0
        self.ninstr = 0

    def _wait(self, eng, sid, val):
        if val <= 0 or self.waited[eng].get(sid, 0) >= val:
            return
        self.waited[eng][sid] = val
        sem = self.semobj[sid]
        self.q[eng].append(lambda E, sem=sem, val=val: E.wait_ge(sem, val))

    def _deps(self, eng, reads, writes):
        deps = {}
        for d in reads:
            for k, v in d.w.items():
                if deps.get(k, 0) < v:
                    deps[k] = v
        for d in writes:
            for k, v in d.w.items():
                if deps.get(k, 0) < v:
                    deps[k] = v
            for k, v in d.r.items():
                if deps.get(k, 0) < v:
                    deps[k] = v
        for k, v in deps.items():
            if eng == 'pe' and k == 'c_pe':
                continue
            self._wait(eng, k, v)

    def _mark(self, sid, v, reads, writes):
        for d in reads:
            if d.r.get(sid, 0) < v:
                d.r[sid] = v
        for d in writes:
            d.w = {sid: v}
            d.r = {}

    def op(self, eng, fn, reads=(), writes=()):
        self._deps(eng, reads, writes)
        self.cnt[eng] += 1
        v = self.cnt[eng]
        sem = self.semobj['c_' + eng]
        rec = _Rec()
        fn(rec)
        name, a, k = rec.call
        self.q[eng].append(lambda E, name=name, a=a, k=k, sem=sem: getattr(E, name)(*a, **k).then_inc(sem, 1))
        self._mark('c_' + eng, v, reads, writes)
        self.ninstr += 1

    def dma(self, queue, out, in_, reads=(), writes=(), **kw):
        self._deps(queue, reads, writes)
        i = self.dcount[queue]
        self.dcount[queue] += 1
        sid = f'd_{queue}_{i % self.K}'
        base = 16 * (i // self.K)
        self._wait(queue, sid, base)
        sem = self.semobj[sid]
        self.q[queue].append(
            lambda E, out=out, in_=in_, sem=sem, kw=kw: E.dma_start(out=out, in_=in_, **kw).then_inc(sem, 16))
        self.ringlast[sid] = base + 16
        self._mark(sid, base + 16, reads, writes)
        self.ninstr += 1

    def barrier(self):
        for e in ENGS:
            for f in ENGS:
                if f != e:
                    self._wait(e, 'c_' + f, self.cnt[f])
            for sid, v in self.ringlast.items():
                self._wait(e, sid, v)

    def finish(self):
        self.barrier()
        nc = self.nc
        q = self.q
        with nc.Block() as blk:
            @blk.sync
            def _(E):
                for f in q['sp']:
                    f(E)

            @blk.tensor
            def _(E):
                for f in q['pe']:
                    f(E)

            @blk.vector
            def _(E):
                for f in q['dve']:
                    f(E)

            @blk.scalar
            def _(E):
                for f in q['act']:
                    f(E)

            @blk.gpsimd
            def _(E):
                for f in q['pool']:
                    f(E)


class Ring:
    def __init__(self, nc, es, name, shape, dtype, n):
        _UID[0] += 1
        self.bufs = [es.enter_context(nc.sbuf_tensor(f'{name}{i}_{_UID[0]}', shape, dtype)) for i in range(n)]
        self.deps = [Dep() for _ in range(n)]
        self.i = 0

    def next(self):
        j = self.i % len(self.bufs)
        self.i += 1
        return self.bufs[j], self.deps[j]


class MK:
    def __init__(self, plan):
        self.plan = plan
        nc = bass.Bass("TRN2", target_bir_lowering=False)
        self.nc = nc
        self.S = Sched(nc)
        self.inp = {}
        self.es_global = ExitStack()

    def din(self, name, shape, dtype=F32):
        t = self.nc.dram_tensor(name, list(shape), dtype, kind="ExternalInput").ap()
        self.inp[name] = t
        return t

    def sb(self, es, name, shape, dtype=F32):
        _UID[0] += 1
        return es.enter_context(self.nc.sbuf_tensor(f"{name}_{_UID[0]}", list(shape), dtype))

    def setup(self):
        nc, S = self.nc, self.S
        g = self.es_global
        self.x = self.din("x", [SEQ, D])
        self.ctx = self.din("ctx", [NCTX, D])
        self.cT_in = self.din("cT", [128, 8, 2])
        self.ada_w = self.din("ada_w", [DEPTH, D, 9 * D])
        self.ada_bT = self.din("ada_bT", [128, DEPTH, 72])
        self.ffn_wg = self.din("ffn_wg", [DEPTH, 2, D, DFF])
        self.ffn_wu = self.din("ffn_wu", [DEPTH, 2, D, DFF])
        self.ffn_wd = self.din("ffn_wd", [DEPTH, 2, DFF, D])
        self.final_gT = self.din("final_gT", [128, 8])
        self.ident_in = self.din("ident", [128, 128])
        self.out = nc.dram_tensor("out", [SEQ, D], F32, kind="ExternalOutput").ap()
        self.da_wqkv = self.din("da_wqkv", [2, D, 3 * D])
        self.da_wo = self.din("da_wo", [2, D, D])
        self.da_lamB = self.din("da_lamB", [2, 128, 256])
        self.da_subln = self.din("da_subln", [2, 128])
        self.gq_wqkv = self.din("gq_wqkv", [1, D, 1536])
        self.gq_wo = self.din("gq_wo", [1, D, D])
        self.gq_gT = self.din("gq_gT", [128, 4])
        self.ropeC_da = self.din("ropeC_da", [128, SEQ])
        self.ropeS_da = self.din("ropeS_da", [128, SEQ])
        self.ropeC_gq = self.din("ropeC_gq", [128, SEQ])
        self.ropeS_gq = self.din("ropeS_gq", [128, SEQ])
        self.rw_wo = self.din("rw_wo", [1, D, D])
        self.rw_wr = self.din("rw_wr", [1, D, D])
        self.rw_wk = self.din("rw_wk", [1, D, D])
        self.rw_wv = self.din("rw_wv", [1, D, D])
        self.rw_w1 = self.din("rw_w1", [1, 2, D, 64])
        self.rw_w2 = self.din("rw_w2", [1, 2, 64, D])
        self.rw_a1 = self.din("rw_a1", [1, 2, D, 64])
        self.rw_a2 = self.din("rw_a2", [1, 2, 64, D])
        self.rw_g1 = self.din("rw_g1", [1, 2, D, 128])
        self.rw_g2 = self.din("rw_g2", [1, 2, 128, D])
        self.rw_vec = self.din("rw_vec", [128, 26, 8])
        self.rw_cst = self.din("rw_cst", [128, 5, 256])
        self.rw_rmask = self.din("rw_rmask", [128, 256])
        self.rw_blk = self.din("rw_blk", [128, 128])
        self.rw_lmask = self.din("rw_lmask", [128, 6, 256])
        self.qkT = nc.dram_tensor("qkT", [16, 128, T], BF16).ap()
        self.vS = nc.dram_tensor("vS", [T, D], BF16).ap()
        self.attnT = nc.dram_tensor("attnT", [8, 128, T], BF16).ap()
        self.hT = nc.dram_tensor("hT", [8, 128, T], F32).ap()
        self.hTv = self.hT.rearrange("c p t -> p c t")
        self.ident = self.sb(g, "ident_sb", [128, 128])
        self.identb = self.sb(g, "identb_sb", [128, 128], BF16)
        self.onesb = self.sb(g, "onesb", [128, 128], BF16)
        self.modT = self.sb(g, "modT", [128, DEPTH * 2, 72])
        self.fgT = self.sb(g, "fgT", [128, 8])
        self.mx = self.sb(g, "mx", [128, 32, 9])
        self.sel = self.sb(g, "sel", [128, 2, 128], BF16)
        self.d_mx = Dep()
        self.d_const = Dep()
        self.d_mod = Dep()
        S.dma('sp', self.ident[:], self.ident_in, writes=[self.d_const])
        S.dma('sp', self.fgT[:], self.final_gT, writes=[self.d_const])
        S.op('dve', lambda E: E.tensor_copy(out=self.identb[:], in_=self.ident[:]), reads=[self.d_const], writes=[self.d_const])
        S.op('dve', lambda E: E.memset(self.onesb[:], 1.0), writes=[self.d_const])
        S.op('dve', lambda E: E.memset(self.sel[:], 0.0), writes=[self.d_const])
        S.op('dve', lambda E: E.memset(self.sel[0:64, 0, :], 1.0), writes=[self.d_const])
        S.op('dve', lambda E: E.memset(self.sel[64:128, 1, :], 1.0), writes=[self.d_const])
        self.ps = [g.enter_context(nc.psum_tensor(f"ps{i}", [128, 512], F32)) for i in range(8)]
        self.dps = [Dep() for _ in range(8)]

    def mod(self, l, w, m, c):
        j = m * 8 + c
        return self.modT[:, l * 2 + w, j:j + 1]

    def stage_init(self):
        nc, S = self.nc, self.S
        with ExitStack() as es:
            xin = Ring(nc, es, "xin", [128, 4, D], F32, 2)
            hsb = Ring(nc, es, "hsb", [128, 8, 512], F32, 2)
            for (off, tt) in TILES:
                ns = tt // 128
                xb, xd = xin.next()
                src = self.ctx if off == 0 else self.x[off - NCTX: off - NCTX + tt, :]
                S.dma('sp', xb[:, 0:ns, :], src.rearrange("(s p) f -> p s f", p=128), writes=[xd])
                hb, hd = hsb.next()
                for c in range(8):
                    for s in range(ns):
                        S.op('pe', lambda E, c=c, s=s, xb=xb: E.transpose(
                            out=self.ps[c][:, s * 128:(s + 1) * 128], in_=xb[:, s, c * 128:(c + 1) * 128], identity=self.ident[:]),
                            reads=[xd, self.d_const], writes=[self.dps[c]])
                    eng = 'dve' if c % 2 == 0 else 'act'
                    if eng == 'dve':
                        S.op('dve', lambda E, c=c, hb=hb, tt=tt: E.tensor_copy(out=hb[:, c, 0:tt], in_=self.ps[c][:, 0:tt]),
                             reads=[self.dps[c]], writes=[hd])
                    else:
                        S.op('act', lambda E, c=c, hb=hb, tt=tt: E.activation(out=hb[:, c, 0:tt], in_=self.ps[c][:, 0:tt], func=AF.Copy),
                             reads=[self.dps[c]], writes=[hd])
                S.dma('pool', self.hTv[:, :, off:off + tt], hb[:, :, 0:tt], reads=[hd])
        S.barrier()

    def stage_ada(self):
        nc, S = self.nc, self.S
        with ExitStack() as es:
            cT = self.sb(es, "cT_sb", [128, 8, 2])
            cact = self.sb(es, "cact", [128, 8, 2])
            abT = self.sb(es, "abT", [128, DEPTH, 72])
            dc = Dep()
            S.dma('sp', cT[:], self.cT_in, writes=[dc])
            S.dma('sp', abT[:], self.ada_bT, writes=[dc])
            S.op('act', lambda E: E.activation(out=cact[:], in_=cT[:], func=AF.Silu), reads=[dc], writes=[dc])
            wr = Ring(nc, es, "adaw", [128, 8, 1152], F32, 2)
            for l in range(DEPTH):
                wv = self.ada_w[l].rearrange("(k p) n -> p k n", p=128)
                bank = l % 2
                for piece in range(8):
                    wb, wd = wr.next()
                    S.dma('sp' if piece % 2 == 0 else 'pool', wb[:], wv[:, :, piece * 1152:(piece + 1) * 1152], writes=[wd])
                    for jj in range(9):
                        j = piece * 9 + jj
                        for k in range(8):
                            S.op('pe', lambda E, wb=wb, jj=jj, k=k, j=j, bank=bank: E.matmul(
                                self.ps[bank][:, 2 * j:2 * j + 2], lhsT=wb[:, k, jj * 128:(jj + 1) * 128], rhs=cact[:, k, :],
                                start=(k == 0), stop=(k == 7)), reads=[wd, dc], writes=[self.dps[bank]])
                for w in range(2):
                    S.op('dve', lambda E, l=l, w=w, bank=bank: E.tensor_tensor(
                        out=self.modT[:, l * 2 + w, :], in0=self.ps[bank][:, w:144:2], in1=abT[:, l, :], op=ALU.add),
                        reads=[self.dps[bank], dc], writes=[self.d_mod])
            for m in (1, 4, 7):
                S.op('dve', lambda E, m=m: E.tensor_scalar(out=self.modT[:, :, m * 8:m * 8 + 8], in0=self.modT[:, :, m * 8:m * 8 + 8],
                                                           scalar1=1.0, scalar2=None, op0=ALU.add),
                     reads=[self.d_mod], writes=[self.d_mod])
            for m in (2, 8):
                S.op('dve', lambda E, m=m: E.tensor_scalar(out=self.modT[:, :, m * 8:m * 8 + 8], in0=self.modT[:, :, m * 8:m * 8 + 8],
                                                           scalar1=0.5, scalar2=None, op0=ALU.mult),
                     reads=[self.d_mod], writes=[self.d_mod])
        S.barrier()

    def make_norm_bufs(self, es):
        nc = self.nc
        nb = {}
        nb['hin'] = Ring(nc, es, "hin", [128, 8, 512], F32, 1)
        nb['sq'] = Ring(nc, es, "sq", [128, 512], BF16, 2)
        nb['tmp'] = Ring(nc, es, "ntmp", [128, 512], F32, 2)
        nb['rs'] = Ring(nc, es, "rs", [128, 512], F32, 2)
        return nb

    def norm_mod(self, nb, off, tt, l, mshift, uT, ud, ps_stat=7, out_f32=False):
        S = self.S
        w = 1 if off == 0 else 0
        hb, hd = nb['hin'].next()
        S.dma('sp', hb[:, :, 0:tt], self.hTv[:, :, off:off + tt], writes=[hd])
        pst, dpst = self.ps[ps_stat], self.dps[ps_stat]
        for c in range(8):
            sq, sqd = nb['sq'].next()
            S.op('act', lambda E, sq=sq, hb=hb, c=c: E.activation(out=sq[:, 0:tt], in_=hb[:, c, 0:tt], func=AF.Square),
                 reads=[hd], writes=[sqd])
            S.op('pe', lambda E, sq=sq, c=c: E.matmul(pst[:, 0:tt], lhsT=self.onesb[:], rhs=sq[:, 0:tt], start=(c == 0), stop=(c == 7)),
                 reads=[sqd, self.d_const], writes=[dpst])
        r1, r1d = nb['rs'].next()
        r2, r2d = nb['rs'].next()
        S.op('act', lambda E: E.activation(out=r1[:, 0:tt], in_=pst[:, 0:tt], func=AF.Sqrt, scale=1.0 / D, bias=self.epsc[:, 0:1]),
             reads=[dpst, self.d_const], writes=[r1d])
        S.op('dve', lambda E: E.reciprocal(out=r2[:, 0:tt], in_=r1[:, 0:tt]), reads=[r1d], writes=[r2d])
        for c in range(8):
            tmp, td = nb['tmp'].next()
            S.op('dve', lambda E, tmp=tmp, c=c: E.tensor_tensor(out=tmp[:, 0:tt], in0=hb[:, c, 0:tt], in1=r2[:, 0:tt], op=ALU.mult),
                 reads=[hd, r2d], writes=[td])
            S.op('act', lambda E, tmp=tmp, c=c: E.activation(out=uT[:, c, 0:tt], in_=tmp[:, 0:tt], func=AF.Identity,
                                                             scale=self.mod(l, w, mshift + 1, c), bias=self.mod(l, w, mshift, c)),
                 reads=[td, self.d_mod], writes=[ud])

    def stage_ffn(self, l, s):
        nc, S = self.nc, self.S
        mb = 0 if s == 0 else 6
        with ExitStack() as es:
            wg = self.sb(es, "wg", [128, 8, DFF], BF16)
            wu = self.sb(es, "wu", [128, 8, DFF], BF16)
            wd = self.sb(es, "wd", [128, 22, D], BF16)
            dwg, dwu, dwd = Dep(), Dep(), Dep()
            gv = self.ffn_wg[l, s].rearrange("(k p) n -> p k n", p=128)
            uv = self.ffn_wu[l, s].rearrange("(k p) n -> p k n", p=128)
            dv = self.ffn_wd[l, s].rearrange("(j p) n -> p j n", p=128)
            for k in range(8):
                S.dma('pool', wg[:, k, :], gv[:, k, :], writes=[dwg])
                S.dma('pool', wu[:, k, :], uv[:, k, :], writes=[dwu])
            for j in range(0, 22, 2):
                S.dma('pool', wd[:, j:j + 2, :], dv[:, j:j + 2, :], writes=[dwd])
            nb = self.make_norm_bufs(es)
            uTr = Ring(nc, es, "uT", [128, 8, 512], BF16, 2)
            sgr = Ring(nc, es, "sg", [128, 512], BF16, 2)
            aT = self.sb(es, "aT", [128, 22, 512], BF16)
            daT = Dep()
            hres = Ring(nc, es, "hres", [128, 512], F32, 2)

            def prologue(i):
                off, tt = TILES[i]
                u, ud = uTr.next()
                self.norm_mod(nb, off, tt, l, mb, u, ud)
                return u, ud

            cur = prologue(0)
            for i, (off, tt) in enumerate(TILES):
                w = 1 if off == 0 else 0
                u, ud = cur
                for j in range(22):
                    bg, bu = (0, 1) if j % 2 == 0 else (2, 3)
                    for k in range(8):
                        S.op('pe', lambda E, j=j, k=k, bg=bg, u=u: E.matmul(self.ps[bg][:, 0:tt], lhsT=wg[:, k, j * 128:(j + 1) * 128],
                                                                        rhs=u[:, k, 0:tt], start=(k == 0), stop=(k == 7)),
                             reads=[dwg, ud], writes=[self.dps[bg]])
                    for k in range(8):
                        S.op('pe', lambda E, j=j, k=k, bu=bu, u=u: E.matmul(self.ps[bu][:, 0:tt], lhsT=wu[:, k, j * 128:(j + 1) * 128],
                                                                        rhs=u[:, k, 0:tt], start=(k == 0), stop=(k == 7)),
                             reads=[dwu, ud], writes=[self.dps[bu]])
                    sg, sgd = sgr.next()
                    S.op('act', lambda E, sg=sg, bg=bg: E.activation(out=sg[:, 0:tt], in_=self.ps[bg][:, 0:tt], func=AF.Silu),
                         reads=[self.dps[bg]], writes=[sgd])
                    S.op('dve', lambda E, sg=sg, bu=bu, j=j: E.tensor_tensor(out=aT[:, j, 0:tt], in0=self.ps[bu][:, 0:tt], in1=sg[:, 0:tt], op=ALU.mult),
                         reads=[self.dps[bu], sgd], writes=[daT])
                    if j == 11 and i + 1 < len(TILES):
                        cur = prologue(i + 1)
                for c in range(8):
                    by = 4 + (c % 2)
                    hr, hrd = hres.next()
                    S.dma('sp', hr[:, 0:tt], self.hT[c, :, off:off + tt], writes=[hrd])
                    for j in range(22):
                        S.op('pe', lambda E, j=j, c=c, by=by: E.matmul(self.ps[by][:, 0:tt], lhsT=wd[:, j, c * 128:(c + 1) * 128],
                                                                    rhs=aT[:, j, 0:tt], start=(j == 0), stop=(j == 21)),
                             reads=[dwd, daT], writes=[self.dps[by]])
                    S.op('dve', lambda E, hr=hr, c=c, by=by, w=w: E.scalar_tensor_tensor(
                        out=hr[:, 0:tt], in0=self.ps[by][:, 0:tt], scalar=self.mod(l, w, mb + 2, c), in1=hr[:, 0:tt], op0=ALU.mult, op1=ALU.add),
                        reads=[self.dps[by], hrd, self.d_mod], writes=[hrd])
                    S.dma('pool', self.hT[c, :, off:off + tt], hr[:, 0:tt], reads=[hrd])
        S.barrier()

    def stage_final(self):
        nc, S = self.nc, self.S
        with ExitStack() as es:
            hin = Ring(nc, es, "fhin", [128, 8, 512], F32, 2)
            sqr = Ring(nc, es, "fsq", [128, 512], BF16, 2)
            rs = Ring(nc, es, "frs", [128, 512], F32, 2)
            xn = Ring(nc, es, "fxn", [128, 512], F32, 3)
            osb = Ring(nc, es, "fosb", [128, 4, D], F32, 2)
            for (off, tt) in TILES[1:]:
                hb, hd = hin.next()
                S.dma('sp', hb[:], self.hTv[:, :, off:off + tt], writes=[hd])
                for c in range(8):
                    sq, sqd = sqr.next()
                    S.op('act', lambda E, sq=sq, hb=hb, c=c: E.activation(out=sq[:], in_=hb[:, c, :], func=AF.Square), reads=[hd], writes=[sqd])
                    S.op('pe', lambda E, sq=sq, c=c: E.matmul(self.ps[7][:], lhsT=self.onesb[:], rhs=sq[:], start=(c == 0), stop=(c == 7)),
                         reads=[sqd, self.d_const], writes=[self.dps[7]])
                r1, r1d = rs.next()
                r2, r2d = rs.next()
                S.op('act', lambda E, r1=r1: E.activation(out=r1[:], in_=self.ps[7][:], func=AF.Sqrt, scale=1.0 / D, bias=self.epsc[:, 0:1]),
                     reads=[self.dps[7], self.d_const], writes=[r1d])
                S.op('dve', lambda E, r1=r1, r2=r2: E.reciprocal(out=r2[:], in_=r1[:]), reads=[r1d], writes=[r2d])
                ob, od = osb.next()
                for c in range(8):
                    x, xd = xn.next()
                    S.op('dve', lambda E, x=x, hb=hb, c=c, r2=r2: E.scalar_tensor_tensor(
                        out=x[:], in0=hb[:, c, :], scalar=self.fgT[:, c:c + 1], in1=r2[:], op0=ALU.mult, op1=ALU.mult),
                        reads=[hd, r2d, self.d_const], writes=[xd])
                    b = c % 4
                    for s in range(4):
                        S.op('pe', lambda E, x=x, s=s, b=b: E.transpose(out=self.ps[b][:, s * 128:(s + 1) * 128], in_=x[:, s * 128:(s + 1) * 128],
                                                                     identity=self.ident[:]),
                             reads=[xd, self.d_const], writes=[self.dps[b]])
                    if c % 2 == 0:
                        S.op('act', lambda E, ob=ob, b=b, c=c: E.activation(
                            out=ob[:, :, c * 128:(c + 1) * 128], in_=self.ps[b][:].rearrange("p (s f) -> p s f", s=4), func=AF.Copy),
                            reads=[self.dps[b]], writes=[od])
                    else:
                        S.op('dve', lambda E, ob=ob, b=b, c=c: E.tensor_copy(
                            out=ob[:, :, c * 128:(c + 1) * 128], in_=self.ps[b][:].rearrange("p (s f) -> p s f", s=4)),
                            reads=[self.dps[b]], writes=[od])
                S.dma('pool', self.out[off - NCTX:off - NCTX + tt, :].rearrange("(s p) f -> p s f", p=128), ob[:], reads=[od])
        S.barrier()

    def stage_attn_proj(self, l, kind):
        nc, S = self.nc, self.S
        j = l // 3
        da = (kind == 'da')
        nqk = 16 if da else 10
        nqkc = nqk * 128
        nvc = 1024 if da else 256
        wsrc = (self.da_wqkv[j] if da else self.gq_wqkv[0]).rearrange("(k p) n -> p k n", p=128)
        ncols = nqkc + nvc
        with ExitStack() as es:
            w = self.sb(es, "aw", [128, 8, ncols], BF16)
            wsw = self.sb(es, "awsw", [128, 8, nqkc], BF16)
            Ct = self.sb(es, "ropeC", [128, SEQ])
            St = self.sb(es, "ropeS", [128, SEQ])
            dw, dtab = Dep(), Dep()
            for k in range(8):
                S.dma('pool', w[:, k, :], wsrc[:, k, :], writes=[dw])
            S.dma('sp', Ct[:], self.ropeC_da if da else self.ropeC_gq, writes=[dtab])
            S.dma('sp', St[:], self.ropeS_da if da else self.ropeS_gq, writes=[dtab])
            hb = 32 if da else 64
            for k in range(8):
                src4 = w[:, k, 0:nqkc].rearrange("p (b h d) -> p b h d", h=2, d=hb)
                dst4 = wsw[:, k, :].rearrange("p (b h d) -> p b h d", h=2, d=hb)
                S.op('dve', lambda E, src4=src4, dst4=dst4: E.tensor_copy(out=dst4[:, :, 0, :], in_=src4[:, :, 1, :]), reads=[dw], writes=[dw])
                S.op('dve', lambda E, src4=src4, dst4=dst4: E.tensor_copy(out=dst4[:, :, 1, :], in_=src4[:, :, 0, :]), reads=[dw], writes=[dw])
            if not da:
                gT = self.sb(es, "gqg", [128, 4])
                S.dma('sp', gT[:], self.gq_gT, writes=[dtab])
            nb = self.make_norm_bufs(es)
            uTr = Ring(nc, es, "auT", [128, 8, 512], BF16, 2)
            t1r = Ring(nc, es, "t1", [128, 512], F32, 2)
            t2r = Ring(nc, es, "t2", [128, 512], F32, 2)
            sqr = Ring(nc, es, "asq", [128, 512], BF16, 2)
            rsr = Ring(nc, es, "ars", [128, 512], F32, 2)
            qor = Ring(nc, es, "qo", [128, 16, 512], BF16, 1)
            vsr = Ring(nc, es, "vsb", [128, 1024], BF16, 2)
            S.op('dve', lambda E: E.memset(self.mx[:], 0.0), writes=[self.d_mx])
            for ti, (off, tt) in enumerate(TILES):
                lat = off != 0
                r0 = off - NCTX
                u, ud = uTr.next()
                self.norm_mod(nb, off, tt, l, 3, u, ud)
                qo, qod = qor.next()
                for ch in range(nqk):
                    bA, bB = (0, 2) if ch % 2 == 0 else (1, 3)
                    cols = slice(ch * 128, (ch + 1) * 128)
                    for k in range(8):
                        S.op('pe', lambda E, k=k, bA=bA, u=u, cols=cols: E.matmul(self.ps[bA][:, 0:tt], lhsT=w[:, k, cols], rhs=u[:, k, 0:tt],
                                                                          start=(k == 0), stop=(k == 7)), reads=[dw, ud], writes=[self.dps[bA]])
                    if lat:
                        for k in range(8):
                            S.op('pe', lambda E, k=k, bB=bB, u=u, cols=cols: E.matmul(self.ps[bB][:, 0:tt], lhsT=wsw[:, k, cols], rhs=u[:, k, 0:tt],
                                                                              start=(k == 0), stop=(k == 7)), reads=[dw, ud], writes=[self.dps[bB]])
                    t1, t1d = t1r.next()
                    t2, t2d = t2r.next()
                    if da:
                        if lat:
                            S.op('dve', lambda E, t1=t1, bA=bA: E.tensor_tensor(out=t1[:, 0:tt], in0=self.ps[bA][:, 0:tt], in1=Ct[:, r0:r0 + tt], op=ALU.mult),
                                 reads=[self.dps[bA], dtab], writes=[t1d])
                            S.op('dve', lambda E, t2=t2, bB=bB: E.tensor_tensor(out=t2[:, 0:tt], in0=self.ps[bB][:, 0:tt], in1=St[:, r0:r0 + tt], op=ALU.mult),
                                 reads=[self.dps[bB], dtab], writes=[t2d])
                            S.op('pool', lambda E, t1=t1, t2=t2, qo=qo, ch=ch: E.tensor_tensor(out=qo[:, ch, 0:tt], in0=t1[:, 0:tt], in1=t2[:, 0:tt], op=ALU.add),
                                 reads=[t1d, t2d], writes=[qod])
                        else:
                            S.op('act', lambda E, qo=qo, ch=ch, bA=bA: E.activation(out=qo[:, ch, 0:tt], in_=self.ps[bA][:, 0:tt], func=AF.Copy),
                                 reads=[self.dps[bA]], writes=[qod])
                    else:
                        gi = 0 if ch < 8 else 2
                        sq, sqd = sqr.next()
                        S.op('act', lambda E, sq=sq, bA=bA: E.activation(out=sq[:, 0:tt], in_=self.ps[bA][:, 0:tt], func=AF.Square),
                             reads=[self.dps[bA]], writes=[sqd])
                        S.op('pe', lambda E, sq=sq: E.matmul(self.ps[4][:, 0:tt], lhsT=self.onesb[:], rhs=sq[:, 0:tt], start=True, stop=True),
                             reads=[sqd, self.d_const], writes=[self.dps[4]])
                        r1, r1d = rsr.next()
                        r2, r2d = rsr.next()
                        S.op('act', lambda E, r1=r1: E.activation(out=r1[:, 0:tt], in_=self.ps[4][:, 0:tt], func=AF.Sqrt, scale=1.0 / 128, bias=self.epsc[:, 0:1]),
                             reads=[self.dps[4], self.d_const], writes=[r1d])
                        S.op('dve', lambda E, r1=r1, r2=r2: E.reciprocal(out=r2[:, 0:tt], in_=r1[:, 0:tt]), reads=[r1d], writes=[r2d])
                        S.op('dve', lambda E, t1=t1, bA=bA, r2=r2: E.tensor_tensor(out=t1[:, 0:tt], in0=self.ps[bA][:, 0:tt], in1=r2[:, 0:tt], op=ALU.mult),
                             reads=[self.dps[bA], r2d], writes=[t1d])
                        if lat:
                            S.op('dve', lambda E, t1=t1, gi=gi: E.scalar_tensor_tensor(out=t1[:, 0:tt], in0=t1[:, 0:tt], scalar=gT[:, gi:gi + 1], in1=Ct[:, r0:r0 + tt],
                                                                                 op0=ALU.mult, op1=ALU.mult), reads=[t1d, dtab], writes=[t1d])
                            S.op('dve', lambda E, t2=t2, bB=bB, r2=r2: E.tensor_tensor(out=t2[:, 0:tt], in0=self.ps[bB][:, 0:tt], in1=r2[:, 0:tt], op=ALU.mult),
                                 reads=[self.dps[bB], r2d], writes=[t2d])
                            S.op('dve', lambda E, t2=t2, gi=gi: E.scalar_tensor_tensor(out=t2[:, 0:tt], in0=t2[:, 0:tt], scalar=gT[:, gi + 1:gi + 2], in1=St[:, r0:r0 + tt],
                                                                                 op0=ALU.mult, op1=ALU.mult), reads=[t2d, dtab], writes=[t2d])
                            S.op('pool', lambda E, t1=t1, t2=t2, qo=qo, ch=ch: E.tensor_tensor(out=qo[:, ch, 0:tt], in0=t1[:, 0:tt], in1=t2[:, 0:tt], op=ALU.add),
                                 reads=[t1d, t2d], writes=[qod])
                        else:
                            S.op('dve', lambda E, t1=t1, gi=gi, qo=qo, ch=ch: E.tensor_scalar(out=qo[:, ch, 0:tt], in0=t1[:, 0:tt], scalar1=gT[:, gi:gi + 1], scalar2=None,
                                                                                      op0=ALU.mult), reads=[t1d, dtab], writes=[qod])
                    sq2, sq2d = sqr.next()
                    S.op('act', lambda E, sq2=sq2, qo=qo, ch=ch: E.activation(out=sq2[:, 0:tt], in_=qo[:, ch, 0:tt], func=AF.Square), reads=[qod], writes=[sq2d])
                    for c in range(2 if da else 1):
                        bM = 5 + c
                        lhs = self.sel[:, c, :] if da else self.onesb[:]
                        S.op('pe', lambda E, sq2=sq2, bM=bM, lhs=lhs: E.matmul(self.ps[bM][:, 0:tt], lhsT=lhs, rhs=sq2[:, 0:tt], start=True, stop=True),
                             reads=[sq2d, self.d_const], writes=[self.dps[bM]])
                        idx = ch * 2 + c if da else ch
                        S.op('dve', lambda E, bM=bM, idx=idx, ti=ti: E.tensor_reduce(out=self.mx[:, idx, ti:ti + 1], in_=self.ps[bM][:, 0:tt], axis=AX.X, op=ALU.max),
                             reads=[self.dps[bM]], writes=[self.d_mx])
                S.dma('sp', self.qkT[0:nqk].rearrange("c p t -> p c t")[:, :, off:off + tt], qo[:, 0:nqk, 0:tt], reads=[qod])
                for s in range(tt // 128):
                    vb, vd = vsr.next()
                    for hv in range(nvc // 512 if nvc >= 512 else 1):
                        wv = min(512, nvc)
                        b = 0 + hv
                        for k in range(8):
                            S.op('pe', lambda E, k=k, b=b, u=u, s=s, hv=hv, wv=wv: E.matmul(
                                self.ps[b][:, 0:wv], lhsT=u[:, k, s * 128:(s + 1) * 128], rhs=w[:, k, nqkc + hv * 512: nqkc + hv * 512 + wv],
                                start=(k == 0), stop=(k == 7)), reads=[dw, ud], writes=[self.dps[b]])
                        if hv == 0:
                            S.op('act', lambda E, vb=vb, b=b, wv=wv: E.activation(out=vb[:, 0:wv], in_=self.ps[b][:, 0:wv], func=AF.Copy),
                                 reads=[self.dps[b]], writes=[vd])
                        else:
                            S.op('dve', lambda E, vb=vb, b=b, hv=hv: E.tensor_copy(out=vb[:, 512:1024], in_=self.ps[b][:, 0:512]),
                                 reads=[self.dps[b]], writes=[vd])
                    S.dma('sp', self.vS[off + s * 128: off + (s + 1) * 128, 0:nvc], vb[:, 0:nvc], reads=[vd])
        S.barrier()

    def stage_attn_core(self, l, kind):
        nc, S = self.nc, self.S
        j = l // 3
        da = (kind == 'da')
        ncomp = 2 if da else 1
        scale = (64 if da else 128) ** -0.5
        lam_init = 0.8 - 0.6 * math.exp(-0.3 * l)
        psTb = self.ps[6][:].bitcast(BF16)
        with ExitStack() as es:
            mq = self.sb(es, "mq", [128, 32])
            negm = self.sb(es, "negm", [128, 16])
            dm = Dep()
            S.op('dve', lambda E: E.tensor_reduce(out=mq[:], in_=self.mx[:], axis=AX.X, op=ALU.max), reads=[self.d_mx], writes=[dm])
            if da:
                S.op('dve', lambda E: E.tensor_tensor(out=negm[:], in0=mq[:, 0:16], in1=mq[:, 16:32], op=ALU.mult), reads=[dm], writes=[dm])
            else:
                for g in range(2):
                    S.op('dve', lambda E, g=g: E.tensor_scalar(out=negm[:, g * 4:(g + 1) * 4], in0=mq[:, g * 4:(g + 1) * 4], scalar1=mq[:, 8 + g:9 + g],
                                                               scalar2=None, op0=ALU.mult), reads=[dm], writes=[dm])
            S.op('act', lambda E: E.activation(out=negm[:], in_=negm[:], func=AF.Sqrt), reads=[dm], writes=[dm])
            S.op('dve', lambda E: E.tensor_scalar(out=negm[:], in0=negm[:], scalar1=-(scale * 1.02), scalar2=None, op0=ALU.mult), reads=[dm], writes=[dm])
            if da:
                lb = self.sb(es, "lamb", [128, 256])
                lt = self.sb(es, "lamt", [128, 128])
                lam = self.sb(es, "lam", [128, 4])
                S.dma('sp', lb[:], self.da_lamB[j], writes=[dm])
                S.op('dve', lambda E: E.tensor_tensor(out=lt[:, 0:64], in0=lb[:, 0:64], in1=lb[:, 64:128], op=ALU.mult), reads=[dm], writes=[dm])
                S.op('dve', lambda E: E.tensor_tensor(out=lt[:, 64:128], in0=lb[:, 128:192], in1=lb[:, 192:256], op=ALU.mult), reads=[dm], writes=[dm])
                S.op('dve', lambda E: E.tensor_reduce(out=lam[:, 0:2], in_=lt[:].rearrange("p (a d) -> p a d", a=2), axis=AX.X, op=ALU.add), reads=[dm], writes=[dm])
                S.op('act', lambda E: E.activation(out=lam[:, 0:2], in_=lam[:, 0:2], func=AF.Exp), reads=[dm], writes=[dm])
                S.op('dve', lambda E: E.tensor_tensor(out=lam[:, 2:3], in0=lam[:, 1:2], in1=lam[:, 0:1], op=ALU.subtract), reads=[dm], writes=[dm])
                S.op('dve', lambda E: E.tensor_scalar(out=lam[:, 3:4], in0=lam[:, 2:3], scalar1=-lam_init, scalar2=None, op0=ALU.add), reads=[dm], writes=[dm])
            KTr = Ring(nc, es, "KT", [128, T], BF16, 2)
            QTr = Ring(nc, es, "QT", [128, T], BF16, 2)
            Vr = Ring(nc, es, "Vh", [128, 34, 129], BF16, 2)
            for vb in Vr.bufs:
                S.op('pool', lambda E, vb=vb: E.memset(vb[:], 1.0), writes=Vr.deps)
            PTr = Ring(nc, es, "PT", [128, 512], BF16, 4)
            ocr = [Ring(nc, es, f"oc{c}", [128, 4, 129], F32, 2) for c in range(ncomp)]
            smr = Ring(nc, es, "sm", [128, 8], F32, 4)
            otr = Ring(nc, es, "ot", [128, 128], F32, 3)
            onr = Ring(nc, es, "on", [128, 128], BF16, 10)
            aor = Ring(nc, es, "ao", [128, 512], BF16, 2)
            pending = []
            for h in range(8):
                kch = (8 + h) if da else (8 + h // 4)
                vch = h if da else h // 4
                KT, KTd = KTr.next()
                QT, QTd = QTr.next()
                Vh, Vd = Vr.next()
                S.dma('sp', KT[:], self.qkT[kch], writes=[KTd])
                S.dma('sp', QT[:], self.qkT[h], writes=[QTd])
                S.dma('pool', Vh[:, :, 0:128], self.vS[:, vch * 128:(vch + 1) * 128].rearrange("(n p) e -> p n e", p=128), writes=[Vd])
                for (off, tt) in TILES:
                    ns = tt // 128
                    nkc = 2 if off == 0 else 34
                    if off == 0 and l == DEPTH - 1:
                        continue
                    ocs = []
                    for c in range(ncomp):
                        rows = slice(64 * c, 64 * c + 64) if da else slice(0, 128)
                        idx = h * 2 + c if da else h
                        SB = (0, 1, 7)

                        def emit_S(kc):
                            bS = SB[kc % 3]
                            S.op('pe', lambda E: E.matmul(
                                self.ps[bS][:, 0:tt], lhsT=KT[rows, kc * 128:(kc + 1) * 128], rhs=QT[rows, off:off + tt], start=True, stop=True),
                                reads=[KTd, QTd], writes=[self.dps[bS]])
                        for kc0 in range(min(3, nkc)):
                            emit_S(kc0)
                        for kc in range(nkc):
                            if pending and c == 0 and kc == min(3, nkc - 1):
                                pending.pop(0)()
                            bS = SB[kc % 3]
                            PT, PTd = PTr.next()
                            S.op('act', lambda E: E.activation(out=PT[:, 0:tt], in_=self.ps[bS][:, 0:tt], func=AF.Exp,
                                                               scale=scale, bias=negm[:, idx:idx + 1]),
                                 reads=[self.dps[bS], dm], writes=[PTd])
                            for s in range(ns):
                                S.op('pe', lambda E: E.matmul(
                                    self.ps[2 + s][:, 0:129], lhsT=PT[:, s * 128:(s + 1) * 128], rhs=Vh[:, kc, :], start=(kc == 0), stop=(kc == nkc - 1)),
                                    reads=[PTd, Vd], writes=[self.dps[2 + s]])
                            if kc + 3 < nkc:
                                emit_S(kc + 3)
                        oc, ocd = ocr[c].next()
                        for s in range(ns):
                            if s % 2 == 0:
                                S.op('act', lambda E, oc=oc, s=s: E.activation(out=oc[:, s, :], in_=self.ps[2 + s][:, 0:129], func=AF.Copy),
                                     reads=[self.dps[2 + s]], writes=[ocd])
                            else:
                                S.op('dve', lambda E, oc=oc, s=s: E.tensor_copy(out=oc[:, s, :], in_=self.ps[2 + s][:, 0:129]),
                                     reads=[self.dps[2 + s]], writes=[ocd])
                        ocs.append((oc, ocd))
                    ons = []
                    for s in range(ns):
                        sm, smd = smr.next()
                        on, ond = onr.next()
                        o0, o0d = ocs[0]
                        S.op('dve', lambda E, sm=sm, o0=o0, s=s: E.reciprocal(out=sm[:, 0:1], in_=o0[:, s, 128:129]), reads=[o0d], writes=[smd])
                        if da:
                            o1, o1d = ocs[1]
                            ot, otd = otr.next()
                            S.op('dve', lambda E, sm=sm, o1=o1, s=s: E.reciprocal(out=sm[:, 1:2], in_=o1[:, s, 128:129]), reads=[o1d, smd], writes=[smd])
                            S.op('dve', lambda E, sm=sm: E.tensor_tensor(out=sm[:, 2:3], in0=sm[:, 1:2], in1=lam[:, 3:4], op=ALU.mult), reads=[smd, dm], writes=[smd])
                            S.op('dve', lambda E, sm=sm, o0=o0, ot=ot, s=s: E.tensor_scalar(out=ot[:], in0=o0[:, s, 0:128], scalar1=sm[:, 0:1], scalar2=None, op0=ALU.mult),
                                 reads=[o0d, smd], writes=[otd])
                            S.op('dve', lambda E, sm=sm, o1=o1, ot=ot, s=s: E.scalar_tensor_tensor(out=ot[:], in0=o1[:, s, 0:128], scalar=sm[:, 2:3], in1=ot[:],
                                                                                             op0=ALU.mult, op1=ALU.add), reads=[o1d, smd, otd], writes=[otd])
                            ot2, ot2d = otr.next()
                            S.op('act', lambda E, ot=ot, ot2=ot2, sm=sm: E.activation(out=ot2[:], in_=ot[:], func=AF.Square, accum_out=sm[:, 3:4]),
                                 reads=[otd, smd], writes=[ot2d, smd])
                            S.op('act', lambda E, sm=sm: E.activation(out=sm[:, 4:5], in_=sm[:, 3:4], func=AF.Sqrt, scale=1.0 / 128, bias=self.epsc[:, 0:1]),
                                 reads=[smd, self.d_const], writes=[smd])
                            S.op('dve', lambda E, sm=sm: E.reciprocal(out=sm[:, 5:6], in_=sm[:, 4:5]), reads=[smd], writes=[smd])
                            S.op('dve', lambda E, sm=sm, ot=ot, on=on: E.tensor_scalar(out=on[:], in0=ot[:], scalar1=sm[:, 5:6], scalar2=None, op0=ALU.mult),
                                 reads=[otd, smd], writes=[ond])
                        else:
                            S.op('dve', lambda E, sm=sm, o0=o0, on=on, s=s: E.tensor_scalar(out=on[:], in0=o0[:, s, 0:128], scalar1=sm[:, 0:1], scalar2=None, op0=ALU.mult),
                                 reads=[o0d, smd], writes=[ond])
                        ons.append((on, ond))

                    def make_flush(ons=ons, h=h, off=off, tt=tt):
                        def f():
                            for s_, (on_, ond_) in enumerate(ons):
                                S.op('pe', lambda E: E.transpose(out=psTb[:, s_ * 128:(s_ + 1) * 128], in_=on_[:], identity=self.identb[:]),
                                     reads=[ond_, self.d_const], writes=[self.dps[6]])
                            ao, aod = aor.next()
                            S.op('act', lambda E: E.activation(out=ao[:, 0:tt], in_=psTb[:, 0:tt], func=AF.Copy), reads=[self.dps[6]], writes=[aod])
                            S.dma('sp', self.attnT[h, :, off:off + tt], ao[:, 0:tt], reads=[aod])
                        return f
                    pending.append(make_flush())
            while pending:
                pending.pop(0)()
        S.barrier()

    def stage_wo(self, l, kind):
        nc, S = self.nc, self.S
        j = l // 3
        lam_init = 0.8 - 0.6 * math.exp(-0.3 * l)
        wsrc = {'da': self.da_wo[j] if kind == 'da' else None, 'gq': self.gq_wo[0], 'rw': self.rw_wo[0]}[kind]
        with ExitStack() as es:
            wst = self.sb(es, "wost", [128, 8, D], F32)
            wo = self.sb(es, "wo", [128, 8, D], BF16)
            dw = Dep()
            S.dma('sp', wst[:], wsrc.rearrange("(k p) n -> p k n", p=128), writes=[dw])
            if kind == 'da':
                sg = self.sb(es, "sublng", [128, 1])
                S.dma('sp', sg[:], self.da_subln[j].rearrange("(p o) -> p o", o=1), writes=[dw])
                S.op('dve', lambda E: E.tensor_scalar(out=sg[:], in0=sg[:], scalar1=(1.0 - lam_init), scalar2=None, op0=ALU.mult), reads=[dw], writes=[dw])
                for k in range(8):
                    S.op('dve', lambda E, k=k: E.tensor_scalar(out=wo[:, k, :], in0=wst[:, k, :], scalar1=sg[:, 0:1], scalar2=None, op0=ALU.mult),
                         reads=[dw], writes=[dw])
            else:
                for k in range(8):
                    S.op('dve' if k % 2 else 'pool', lambda E, k=k: E.tensor_copy(out=wo[:, k, :], in_=wst[:, k, :]), reads=[dw], writes=[dw])
            atr = Ring(nc, es, "at", [128, 8, 512], BF16, 2)
            hres = Ring(nc, es, "whres", [128, 512], F32, 3)
            for (off, tt) in TILES:
                if off == 0 and l == DEPTH - 1:
                    continue
                w = 1 if off == 0 else 0
                at, atd = atr.next()
                S.dma('sp', at[:, :, 0:tt], self.attnT.rearrange("c p t -> p c t")[:, :, off:off + tt], writes=[atd])
                for c in range(8):
                    b = c % 4
                    hr, hrd = hres.next()
                    S.dma('sp', hr[:, 0:tt], self.hT[c, :, off:off + tt], writes=[hrd])
                    for k in range(8):
                        S.op('pe', lambda E, k=k, c=c, b=b, at=at: E.matmul(self.ps[b][:, 0:tt], lhsT=wo[:, k, c * 128:(c + 1) * 128], rhs=at[:, k, 0:tt],
                                                                     start=(k == 0), stop=(k == 7)), reads=[dw, atd], writes=[self.dps[b]])
                    S.op('dve', lambda E, hr=hr, c=c, b=b, w=w: E.scalar_tensor_tensor(
                        out=hr[:, 0:tt], in0=self.ps[b][:, 0:tt], scalar=self.mod(l, w, 5, c), in1=hr[:, 0:tt], op0=ALU.mult, op1=ALU.add),
                        reads=[self.dps[b], hrd, self.d_mod], writes=[hrd])
                    S.dma('pool', self.hT[c, :, off:off + tt], hr[:, 0:tt], reads=[hrd])
        S.barrier()

    def stage_rwkv(self, l):
        nc, S = self.nc, self.S
        RT = 256
        NCH = RT // 64
        NRT = T // RT
        GN_EPS = 64e-5
        uTs = nc.dram_tensor("rw_uT", [8, 128, T], F32).ap()
        o0T = nc.dram_tensor("rw_o0T", [8, 128, T], F32).ap()
        with ExitStack() as es:
            nb = self.make_norm_bufs(es)
            ur = Ring(nc, es, "rwu", [128, 8, 512], F32, 2)
            for (off, tt) in TILES:
                u, ud = ur.next()
                self.norm_mod(nb, off, tt, l, 3, u, ud)
                S.dma('pool', uTs.rearrange("c p t -> p c t")[:, :, off:off + tt], u[:, :, 0:tt], reads=[ud])
        S.barrier()
        with ExitStack() as es:
            class B:
                __slots__ = ('ap', 'd')

                def __init__(s, ap, d=None):
                    s.ap = ap
                    s.d = d if d is not None else Dep()

            def newb(name, shape, dt=F32):
                t = self.sb(es, name, shape, dt)
                return B(t[:], Dep()), t

            def TTo(eng, o, a, b, op):
                S.op(eng, lambda E: E.tensor_tensor(out=o.ap, in0=a.ap, in1=b.ap, op=op), reads=[a.d, b.d], writes=[o.d])

            def TS(eng, o, a, s1, s2, op0, op1=None, extra=()):
                if op1 is None:
                    S.op(eng, lambda E: E.tensor_scalar(out=o.ap, in0=a.ap, scalar1=s1, scalar2=None, op0=op0), reads=[a.d, *extra], writes=[o.d])
                else:
                    S.op(eng, lambda E: E.tensor_scalar(out=o.ap, in0=a.ap, scalar1=s1, scalar2=s2, op0=op0, op1=op1), reads=[a.d, *extra], writes=[o.d])

            def STT(o, a, sc, b, op0, op1, extra=()):
                S.op('dve', lambda E: E.scalar_tensor_tensor(out=o.ap, in0=a.ap, scalar=sc, in1=b.ap, op0=op0, op1=op1), reads=[a.d, b.d, *extra], writes=[o.d])

            def ACT(o, a, func, scale=1.0, bias=None, extra=()):
                if bias is None:
                    S.op('act', lambda E: E.activation(out=o.ap, in_=a.ap, func=func, scale=scale), reads=[a.d, *extra], writes=[o.d])
                else:
                    S.op('act', lambda E: E.activation(out=o.ap, in_=a.ap, func=func, scale=scale, bias=bias), reads=[a.d, *extra], writes=[o.d])

            def MM(o, lhsT, rhs, start, stop, reads):
                S.op('pe', lambda E: E.matmul(o.ap, lhsT=lhsT, rhs=rhs, start=start, stop=stop), reads=reads, writes=[o.d])

            PSB = [B(self.ps[i][:, 0:RT], self.dps[i]) for i in range(8)]
            dW = Dep()
            def loadw(name, src, shape, view):
                t = self.sb(es, name, shape, BF16)
                S.dma('pool', t[:], view, writes=[dW])
                return t
            wr = loadw("rwr", self.rw_wr, [128, 8, D], self.rw_wr[0].rearrange("(k p) n -> p k n", p=128))
            wk = loadw("rwk", self.rw_wk, [128, 8, D], self.rw_wk[0].rearrange("(k p) n -> p k n", p=128))
            wv = loadw("rwv", self.rw_wv, [128, 8, D], self.rw_wv[0].rearrange("(k p) n -> p k n", p=128))
            def allocw(name, shape):
                return self.sb(es, name, shape, BF16)
            w1s, a1s, g1s = allocw("rw1", [128, 8, 64]), allocw("ra1", [128, 8, 64]), allocw("rg1", [128, 8, 128])
            w2s, a2s, g2s = allocw("rw2", [64, D]), allocw("ra2", [64, D]), allocw("rg2", [128, D])
            w1 = [w1s, w1s]; a1 = [a1s, a1s]; g1 = [g1s, g1s]
            w2 = [w2s, w2s]; a2 = [a2s, a2s]; g2 = [g2s, g2s]

            def load_dir_weights(d):
                S.dma('pool', w1s[:], self.rw_w1[0, d].rearrange("(k p) n -> p k n", p=128), writes=[dW])
                S.dma('pool', a1s[:], self.rw_a1[0, d].rearrange("(k p) n -> p k n", p=128), writes=[dW])
                S.dma('pool', g1s[:], self.rw_g1[0, d].rearrange("(k p) n -> p k n", p=128), writes=[dW])
                S.dma('pool', w2s[:], self.rw_w2[0, d], writes=[dW])
                S.dma('pool', a2s[:], self.rw_a2[0, d], writes=[dW])
                S.dma('pool', g2s[:], self.rw_g2[0, d], writes=[dW])
            vec = self.sb(es, "rwvec", [128, 26, 8])
            cst = self.sb(es, "rwcst", [128, 5, RT])
            rmask = self.sb(es, "rwrm", [128, RT])
            blk = self.sb(es, "rwblk", [128, 128])
            blkb = self.sb(es, "rwblkb", [128, 128], BF16)
            coef0 = self.sb(es, "rwc0", [128, 6, 8])
            gneps = self.sb(es, "gneps", [128, 1])
            dV = Dep()
            S.dma('sp', vec[:], self.rw_vec, writes=[dV])
            S.dma('sp', cst[:], self.rw_cst, writes=[dV])
            S.dma('sp', rmask[:], self.rw_rmask, writes=[dV])
            lmask = self.sb(es, "rwlm", [128, 6, RT], BF16)
            S.dma('pool', lmask[:], self.rw_lmask, writes=[dV])
            S.dma('sp', blk[:], self.rw_blk, writes=[dV])
            S.op('dve', lambda E: E.tensor_copy(out=blkb[:], in_=blk[:]), reads=[dV], writes=[dV])
            S.op('dve', lambda E: E.memset(gneps[:], GN_EPS), writes=[dV])
            S.op('dve', lambda E: E.tensor_tensor(out=coef0[:], in0=vec[:, 0:6, :], in1=vec[:, 6:12, :], op=ALU.add), reads=[dV], writes=[dV])
            S.op('dve', lambda E: E.tensor_scalar(out=coef0[:], in0=coef0[:], scalar1=-1.0, scalar2=1.0, op0=ALU.mult, op1=ALU.add), reads=[dV], writes=[dV])
            zt, _ = newb("zt", [128, 8, RT + 2])
            ztt = _
            xm = [newb(f"xm{i}", [128, 8, RT], BF16) for i in range(6)]
            lw = [newb("lwT", [64, RT], BF16), newb("laT", [64, RT], BF16), newb("lgT", [128, RT], BF16)]
            tmpr = Ring(nc, es, "rwt", [128, RT], F32, 21)

            def tmp():
                t, d = tmpr.next()
                return B(t[:], d)
            tbr = Ring(nc, es, "rwtb", [128, RT], BF16, 12)

            def tmpb():
                t, d = tbr.next()
                return B(t[:], d)
            AT = [newb(f"AT{p}", [128, RT], BF16) for p in range(8)]
            RTl = [newb(f"RTl{p}", [128, RT], BF16) for p in range(8)]
            Aak = [newb(f"Aak{p}", [128, RT], BF16) for p in range(8)]
            Arb = [newb(f"Arb{p}", [128, RT], BF16) for p in range(8)]
            Ark = [newb(f"Ark{p}", [128, RT], BF16) for p in range(8)]
            Gm = [newb(f"Gm{p}", [128, RT]) for p in range(8)]
            Bst = [newb(f"Bst{p}", [128, RT], BF16) for p in range(8)]
            Kst = [newb(f"Kst{p}", [128, RT], BF16) for p in range(8)]
            PC = [newb(f"PC{p}", [128, NCH]) for p in range(8)]
            bon = [newb(f"bon{p}", [128, RT]) for p in range(8)]
            gg = [newb(f"gg{p}", [128, RT], BF16) for p in range(8)]
            Ysb, Yt = newb("Ysb", [128, 8, RT])
            Vst, Vstt = newb("Vst", [128, NCH, 512], BF16)
            Sf, Sft = newb("Sf", [128, 8, 64])
            Sb, Sbt = newb("Sbb", [128, 8, 64], BF16)
            Xsb, Xt = newb("Xsb", [128, 512])
            Nf = [newb("Nf0", [128, RT]), newb("Nf1", [128, RT])]
            NTf = [newb("NTf0", [128, RT]), newb("NTf1", [128, RT])]
            Usb, Ut = newb("Usb", [128, 512], BF16)
            o0r = Ring(nc, es, "rwo0", [128, RT], F32, 2)
            obr = Ring(nc, es, "rwob", [128, RT], BF16, 2)

            def blockmm32(o, lt, rt, rd):
                for ch in range(NCH):
                    for hd in range(2):
                        rs_ = slice(64 * hd, 64 * hd + 64)
                        cs_ = slice(ch * 64, (ch + 1) * 64)
                        S.op('pe', lambda E: E.matmul(o.ap[rs_, cs_], lhsT=lt[rs_, cs_], rhs=rt[rs_, cs_], start=True, stop=True), reads=rd, writes=[o.d])

            def chain_pairs(prs):
                II = B(cst[:, 4, :], dV)
                Jd, JTd = {}, {}
                for p_ in prs:
                    J, JT = tmp(), tmp()
                    nf, ntf = Nf[p_ % 2][0], NTf[p_ % 2][0]
                    S.op('dve', lambda E: E.tensor_tensor(out=J.ap, in0=nf.ap, in1=lmask[:, 0, :], op=ALU.mult), reads=[nf.d, dV], writes=[J.d])
                    TTo('pool', J, J, II, ALU.add)
                    S.op('dve', lambda E: E.tensor_tensor(out=JT.ap, in0=ntf.ap, in1=lmask[:, 0, :], op=ALU.mult), reads=[ntf.d, dV], writes=[JT.d])
                    TTo('pool', JT, JT, II, ALU.add)
                    Jd[p_], JTd[p_] = J, JT
                for lev in range(1, 6):
                    Nm, NTm, T1s, T1t = {}, {}, {}, {}
                    for p_ in prs:
                        pbp = [PSB[(4 * (p_ % 2)) + q] for q in range(4)]
                        nf, ntf = Nf[p_ % 2][0], NTf[p_ % 2][0]
                        Nm[p_], NTm[p_], T1s[p_] = tmp(), tmp(), tmp()
                        S.op('pool', lambda E: E.tensor_tensor(out=Nm[p_].ap, in0=nf.ap, in1=lmask[:, lev, :], op=ALU.mult), reads=[nf.d, dV], writes=[Nm[p_].d])
                        S.op('dve', lambda E: E.tensor_tensor(out=NTm[p_].ap, in0=ntf.ap, in1=lmask[:, lev, :], op=ALU.mult), reads=[ntf.d, dV], writes=[NTm[p_].d])
                        blockmm32(pbp[3], NTm[p_].ap, Jd[p_].ap, [NTm[p_].d, Jd[p_].d])
                        ACT(T1s[p_], pbp[3], AF.Copy)
                    if lev < 5:
                        for p_ in prs:
                            pbp = [PSB[(4 * (p_ % 2)) + q] for q in range(4)]
                            T1t[p_] = tmp()
                            blockmm32(pbp[1], Nm[p_].ap, JTd[p_].ap, [Nm[p_].d, JTd[p_].d])
                            S.op('dve', lambda E: E.tensor_copy(out=T1t[p_].ap, in_=pbp[1].ap), reads=[pbp[1].d], writes=[T1t[p_].d])
                    newJ, newJT = {}, {}
                    for p_ in prs:
                        pbp = [PSB[(4 * (p_ % 2)) + q] for q in range(4)]
                        blockmm32(pbp[0], JTd[p_].ap, T1s[p_].ap, [JTd[p_].d, T1s[p_].d])
                        Jn = tmp() if lev < 5 else Gm[p_][0]
                        TTo('dve', Jn, pbp[0], Jd[p_], ALU.add)
                        newJ[p_] = Jn
                    if lev < 5:
                        for p_ in prs:
                            pbp = [PSB[(4 * (p_ % 2)) + q] for q in range(4)]
                            blockmm32(pbp[2], Jd[p_].ap, T1t[p_].ap, [Jd[p_].d, T1t[p_].d])
                            JTn = tmp()
                            TTo('dve', JTn, pbp[2], JTd[p_], ALU.add)
                            newJT[p_] = JTn
                    for p_ in prs:
                        Jd[p_] = newJ[p_]
                        if lev < 5:
                            JTd[p_] = newJT[p_]

            def vcol(n, pr):
                return vec[:, n, pr:pr + 1]

            for d in range(2):
                load_dir_weights(d)
                S.op('dve', lambda E: E.memset(Sft[:], 0.0), reads=[Sf.d], writes=[Sf.d])
                S.op('dve', lambda E: E.memset(Sbt[:], 0.0), reads=[Sb.d], writes=[Sb.d])
                order = list(range(NRT)) if d == 0 else [0] + list(range(NRT - 1, 0, -1))
                mS, mI = (0, 1) if d == 0 else (2, 3)
                mST = 2 if d == 0 else 0
                for ti in order:
                    off = ti * RT
                    if ti == 0 and l == DEPTH - 1:
                        pass
                    seq_lo, seq_hi = (0, NCTX) if ti == 0 else (NCTX, T)
                    S.op('dve', lambda E: E.memset(ztt[:, :, 0:1], 0.0), reads=[zt.d], writes=[zt.d])
                    S.op('dve', lambda E: E.memset(ztt[:, :, RT + 1:RT + 2], 0.0), reads=[zt.d], writes=[zt.d])
                    lo = off - 1 if off > seq_lo else off
                    hi = off + RT + 1 if off + RT < seq_hi else off + RT
                    S.dma('sp', ztt[:, :, 1 - (off - lo): 1 + RT + (hi - off - RT)], uTs.rearrange("c p t -> p c t")[:, :, lo:hi], reads=[zt.d], writes=[zt.d])
                    for i in range(6):
                        for c in range(8):
                            t1 = tmp()
                            S.op('pool', lambda E: E.tensor_scalar(out=t1.ap, in0=ztt[:, c, 1:RT + 1], scalar1=coef0[:, i, c:c + 1], scalar2=None, op0=ALU.mult),
                                 reads=[zt.d, dV], writes=[t1.d])
                            S.op('dve', lambda E: E.scalar_tensor_tensor(out=t1.ap, in0=ztt[:, c, 0:RT], scalar=vec[:, i, c:c + 1], in1=t1.ap, op0=ALU.mult, op1=ALU.add),
                                 reads=[zt.d, dV, t1.d], writes=[t1.d])
                            S.op('dve', lambda E: E.scalar_tensor_tensor(out=xm[i][1][:, c, :], in0=ztt[:, c, 2:RT + 2], scalar=vec[:, 6 + i, c:c + 1], in1=t1.ap,
                                                                         op0=ALU.mult, op1=ALU.add), reads=[zt.d, dV, t1.d], writes=[xm[i][0].d])
                    for k in range(8):
                        MM(B(self.ps[0][0:64, 0:RT], self.dps[0]), w1[d][:, k, :], xm[1][1][:, k, :], k == 0, k == 7, [dW, xm[1][0].d])
                    ACT(lw[0][0], B(self.ps[0][0:64, 0:RT], self.dps[0]), AF.Tanh)
                    for k in range(8):
                        MM(B(self.ps[1][0:64, 0:RT], self.dps[1]), a1[d][:, k, :], xm[4][1][:, k, :], k == 0, k == 7, [dW, xm[4][0].d])
                    ACT(lw[1][0], B(self.ps[1][0:64, 0:RT], self.dps[1]), AF.Copy)
                    for k in range(8):
                        MM(PSB[2], g1[d][:, k, :], xm[5][1][:, k, :], k == 0, k == 7, [dW, xm[5][0].d])
                    ACT(lw[2][0], PSB[2], AF.Sigmoid)
                    wv4 = [wv[:, k, :].rearrange("p (pr hd v) -> p pr hd v", hd=2, v=64) for k in range(8)]
                    for ch in range(NCH):
                        for hd in range(2):
                            for k in range(8):
                                S.op('pe', lambda E: E.matmul(self.ps[3][64 * hd:64 * hd + 64, 0:512], lhsT=xm[3][1][:, k, ch * 64:(ch + 1) * 64],
                                                              rhs=wv4[k][:, :, hd, :], start=(k == 0), stop=(k == 7)),
                                     reads=[dW, xm[3][0].d], writes=[self.dps[3]])
                        S.op('act', lambda E: E.activation(out=Vstt[:, ch, :], in_=self.ps[3][:, 0:512], func=AF.Copy), reads=[self.dps[3]], writes=[Vst.d])
                    def stageA(pr):
                        cols = slice(pr * 128, (pr + 1) * 128)
                        pb = [PSB[(4 * (pr % 2)) + q] for q in range(4)]
                        for k in range(8):
                            MM(pb[0], wr[:, k, cols], xm[0][1][:, k, :], k == 0, k == 7, [dW, xm[0][0].d])
                        for k in range(8):
                            MM(pb[1], wk[:, k, cols], xm[2][1][:, k, :], k == 0, k == 7, [dW, xm[2][0].d])
                        for k in range(8):
                            MM(pb[2], wv[:, k, cols], xm[3][1][:, k, :], k == 0, k == 7, [dW, xm[3][0].d])
                        r_, k_, v_ = tmp(), tmp(), tmp()
                        ACT(r_, pb[0], AF.Copy)
                        ACT(k_, pb[1], AF.Copy)
                        ACT(v_, pb[2], AF.Copy)
                        MM(pb[3], w2[d][:, cols], lw[0][1][:], True, True, [dW, lw[0][0].d])
                        sig = tmp()
                        ACT(sig, pb[3], AF.Sigmoid, bias=vcol(12 + 6 * d + 0, pr), extra=[dV])
                        MM(pb[0], a2[d][:, cols], lw[1][1][:], True, True, [dW, lw[1][0].d])
                        asig = tmp()
                        ACT(asig, pb[0], AF.Sigmoid, bias=vcol(12 + 6 * d + 1, pr), extra=[dV])
                        MM(pb[1], g2[d][:, cols], lw[2][1][:], True, True, [dW, lw[2][0].d])
                        ACT(gg[pr][0], pb[1], AF.Copy)
                        logw = tmp()
                        TS('dve', logw, sig, -0.6065306597126334, None, ALU.mult)
                        kkr = tmp()
                        TS('dve', kkr, k_, vcol(12 + 6 * d + 2, pr), None, ALU.mult, extra=[dV])
                        sqb = tmpb()
                        ACT(sqb, kkr, AF.Square)
                        MM(pb[2], blkb[:], sqb.ap, True, True, [dV, sqb.d])
                        nrm = tmp()
                        ACT(nrm, pb[2], AF.Sqrt)
                        TS('dve', nrm, nrm, 1e-12, None, ALU.max)
                        S.op('dve', lambda E: E.reciprocal(out=nrm.ap, in_=nrm.ap), reads=[nrm.d], writes=[nrm.d])
                        kk = tmp()
                        TTo('dve', kk, kkr, nrm, ALU.mult)
                        kd = tmp()
                        TS('dve', kd, asig, -1.0, vcol(12 + 6 * d + 3, pr), ALU.add, ALU.mult, extra=[dV])
                        STT(kd, kd, 1.0, k_, ALU.add, ALU.mult)
                        rk = tmpb()
                        STT(rk, r_, vcol(24 + d, pr), kd, ALU.mult, ALU.mult, extra=[dV])
                        MM(pb[3], blkb[:], rk.ap, True, True, [dV, rk.d])
                        TTo('dve', bon[pr][0], pb[3], v_, ALU.mult)
                        bsc = tmp()
                        TTo('pool', bsc, kk, asig, ALU.mult)
                        L = tmp()
                        S.op('dve', lambda E: E.tensor_tensor_scan(out=L.ap, data0=rmask[:], data1=logw.ap, initial=0.0, op0=ALU.mult, op1=ALU.add),
                             reads=[logw.d, dV], writes=[L.d])
                        L3 = L.ap.rearrange("p (c t) -> p c t", t=64)
                        tot = L3[:, :, 63:64]
                        Li, Le = tmp(), tmp()
                        if d == 0:
                            S.op('pool', lambda E: E.tensor_copy(out=Li.ap, in_=L.ap), reads=[L.d], writes=[Li.d])
                        else:
                            S.op('dve', lambda E: E.tensor_tensor(out=Li.ap.rearrange("p (c t) -> p c t", t=64), in0=tot.broadcast_to([128, NCH, 64]), in1=L3, op=ALU.subtract),
                                 reads=[L.d], writes=[Li.d])
                            TTo('dve', Li, Li, logw, ALU.add)
                        TTo('pool', Le, Li, logw, ALU.subtract)
                        E1, E2, E3, E4 = tmp(), tmp(), tmp(), tmp()
                        ACT(E1, Le, AF.Exp)
                        ACT(E2, Li, AF.Exp)
                        ACT(E3, Li, AF.Exp, scale=-1.0)
                        D4 = tmp()
                        S.op('dve', lambda E: E.tensor_tensor(out=D4.ap.rearrange("p (c t) -> p c t", t=64), in0=tot.broadcast_to([128, NCH, 64]),
                                                              in1=Li.ap.rearrange("p (c t) -> p c t", t=64), op=ALU.subtract), reads=[L.d, Li.d], writes=[D4.d])
                        ACT(E4, D4, AF.Exp)
                        S.op('act', lambda E: E.activation(out=PC[pr][1][:], in_=L3[:, :, 63], func=AF.Exp), reads=[L.d], writes=[PC[pr][0].d])
                        STT(AT[pr][0], kk, -1.0, E1, ALU.mult, ALU.mult)
                        TTo('pool', RTl[pr][0], r_, E2, ALU.mult)
                        Bt, Kt, Bh, Kh = tmpb(), tmpb(), tmpb(), tmpb()
                        TTo('dve', Bt, bsc, E3, ALU.mult)
                        TTo('pool', Kt, kd, E3, ALU.mult)
                        TTo('dve', Bh, bsc, E4, ALU.mult)
                        TTo('pool', Kh, kd, E4, ALU.mult)

                        return (pb, Bt, Kt, Bh, Kh)

                    def stageM(pr, pb, Bt, Kt, Bh, Kh):
                        def blockmm(o, lt, rt, rd):
                            for ch in range(NCH):
                                for hd in range(2):
                                    rs_ = slice(64 * hd, 64 * hd + 64)
                                    cs_ = slice(ch * 64, (ch + 1) * 64)
                                    S.op('pe', lambda E: E.matmul(o.ap[rs_, cs_], lhsT=lt[rs_, cs_], rhs=rt[rs_, cs_], start=True, stop=True), reads=rd, writes=[o.d])
                        blockmm(pb[0], Bt.ap, AT[pr][0].ap, [Bt.d, AT[pr][0].d])
                        S.op('dve', lambda E: E.tensor_tensor(out=Nf[pr % 2][0].ap, in0=pb[0].ap, in1=cst[:, mS, :], op=ALU.mult), reads=[pb[0].d, dV], writes=[Nf[pr % 2][0].d])
                        blockmm(pb[1], AT[pr][0].ap, Bt.ap, [Bt.d, AT[pr][0].d])
                        S.op('dve', lambda E: E.tensor_tensor(out=NTf[pr % 2][0].ap, in0=pb[1].ap, in1=cst[:, mST, :], op=ALU.mult), reads=[pb[1].d, dV], writes=[NTf[pr % 2][0].d])
                        blockmm(pb[2], Kt.ap, AT[pr][0].ap, [Kt.d, AT[pr][0].d])
                        S.op('dve', lambda E: E.tensor_tensor(out=Aak[pr][0].ap, in0=pb[2].ap, in1=cst[:, mS, :], op=ALU.mult), reads=[pb[2].d, dV], writes=[Aak[pr][0].d])
                        blockmm(pb[3], Bt.ap, RTl[pr][0].ap, [Bt.d, RTl[pr][0].d])
                        S.op('dve', lambda E: E.tensor_tensor(out=Arb[pr][0].ap, in0=pb[3].ap, in1=cst[:, mI, :], op=ALU.mult), reads=[pb[3].d, dV], writes=[Arb[pr][0].d])
                        blockmm(pb[0], Kt.ap, RTl[pr][0].ap, [Kt.d, RTl[pr][0].d])
                        S.op('dve', lambda E: E.tensor_tensor(out=Ark[pr][0].ap, in0=pb[0].ap, in1=cst[:, mI, :], op=ALU.mult), reads=[pb[0].d, dV], writes=[Ark[pr][0].d])
                        for (src_, dst, bank) in ((Bh, Bst[pr][0], pb[1]), (Kh, Kst[pr][0], pb[2])):
                            for ch in range(NCH):
                                for hd in range(2):
                                    rs_ = slice(64 * hd, 64 * hd + 64)
                                    cs_ = slice(ch * 64, (ch + 1) * 64)
                                    S.op('pe', lambda E: E.matmul(bank.ap[rs_, cs_], lhsT=src_.ap[rs_, cs_], rhs=self.identb[rs_, rs_], start=True, stop=True),
                                         reads=[src_.d, self.d_const], writes=[bank.d])
                            ACT(dst, bank, AF.Copy)
                    prevA = stageA(0)
                    for pr in range(8):
                        nxtA = stageA(pr + 1) if pr + 1 < 8 else None
                        stageM(pr, *prevA)
                        if pr % 2 == 1:
                            chain_pairs([pr - 1, pr])
                        prevA = nxtA
                    chs = list(range(NCH)) if d == 0 else list(range(NCH - 1, -1, -1))
                    pX, pU, pY, pS = PSB[0], PSB[1], PSB[2], PSB[3]
                    pXf = B(self.ps[0][:, 0:512], self.dps[0])
                    pUf = B(self.ps[1][:, 0:512], self.dps[1])
                    pYf = B(self.ps[2][:, 0:512], self.dps[2])
                    pSf = B(self.ps[3][:, 0:512], self.dps[3])
                    for ch in chs:
                        cs_ = slice(ch * 64, (ch + 1) * 64)
                        for pr in range(8):
                            ps_ = slice(pr * 64, (pr + 1) * 64)
                            for hd in range(2):
                                rs_ = slice(64 * hd, 64 * hd + 64)
                                S.op('pe', lambda E: E.matmul(pXf.ap[rs_, ps_], lhsT=AT[pr][0].ap[rs_, cs_], rhs=Sbt[rs_, pr, :], start=True, stop=False),
                                     reads=[AT[pr][0].d, Sb.d], writes=[pXf.d])
                                S.op('pe', lambda E: E.matmul(pXf.ap[rs_, ps_], lhsT=Aak[pr][0].ap[rs_, cs_], rhs=Vstt[rs_, ch, ps_], start=False, stop=True),
                                     reads=[Aak[pr][0].d, Vst.d], writes=[pXf.d])
                        S.op('act', lambda E: E.activation(out=Xt[:], in_=pXf.ap, func=AF.Copy), reads=[pXf.d], writes=[Xsb.d])
                        for pr in range(8):
                            ps_ = slice(pr * 64, (pr + 1) * 64)
                            for hd in range(2):
                                rs_ = slice(64 * hd, 64 * hd + 64)
                                S.op('pe', lambda E: E.matmul(pUf.ap[rs_, ps_], lhsT=Gm[pr][0].ap[rs_, cs_], rhs=Xt[rs_, ps_], start=True, stop=True),
                                     reads=[Gm[pr][0].d, Xsb.d], writes=[pUf.d])
                        S.op('dve', lambda E: E.tensor_copy(out=Ut[:], in_=pUf.ap), reads=[pUf.d], writes=[Usb.d])
                        for pr in range(8):
                            ps_ = slice(pr * 64, (pr + 1) * 64)
                            for hd in range(2):
                                rs_ = slice(64 * hd, 64 * hd + 64)
                                S.op('pe', lambda E: E.matmul(pYf.ap[rs_, ps_], lhsT=Sbt[rs_, pr, :], rhs=RTl[pr][0].ap[rs_, cs_], start=True, stop=False),
                                     reads=[RTl[pr][0].d, Sb.d], writes=[pYf.d])
                                S.op('pe', lambda E: E.matmul(pYf.ap[rs_, ps_], lhsT=Ut[rs_, ps_], rhs=Arb[pr][0].ap[rs_, cs_], start=False, stop=False),
                                     reads=[Arb[pr][0].d, Usb.d], writes=[pYf.d])
                                S.op('pe', lambda E: E.matmul(pYf.ap[rs_, ps_], lhsT=Vstt[rs_, ch, ps_], rhs=Ark[pr][0].ap[rs_, cs_], start=False, stop=True),
                                     reads=[Ark[pr][0].d, Vst.d], writes=[pYf.d])
                        S.op('act', lambda E: E.activation(out=Yt[:, :, cs_], in_=pYf.ap.rearrange("p (a t) -> p a t", t=64), func=AF.Copy), reads=[pYf.d], writes=[Ysb.d])
                        for pr in range(8):
                            ps_ = slice(pr * 64, (pr + 1) * 64)
                            for hd in range(2):
                                rs_ = slice(64 * hd, 64 * hd + 64)
                                S.op('pe', lambda E: E.matmul(pSf.ap[rs_, ps_], lhsT=Bst[pr][0].ap[rs_, cs_], rhs=Ut[rs_, ps_], start=True, stop=False),
                                     reads=[Bst[pr][0].d, Usb.d], writes=[pSf.d])
                                S.op('pe', lambda E: E.matmul(pSf.ap[rs_, ps_], lhsT=Kst[pr][0].ap[rs_, cs_], rhs=Vstt[rs_, ch, ps_], start=False, stop=True),
                                     reads=[Kst[pr][0].d, Vst.d], writes=[pSf.d])
                        for pr in range(8):
                            ps_ = slice(pr * 64, (pr + 1) * 64)
                            S.op('dve', lambda E: E.scalar_tensor_tensor(out=Sft[:, pr, :], in0=Sft[:, pr, :], scalar=PC[pr][1][:, ch:ch + 1], in1=pSf.ap[:, ps_],
                                                                         op0=ALU.mult, op1=ALU.add), reads=[Sf.d, PC[pr][0].d, pSf.d], writes=[Sf.d])
                        S.op('dve', lambda E: E.tensor_copy(out=Sbt[:], in_=Sft[:]), reads=[Sf.d], writes=[Sb.d])
                    if ti == 0 and l == DEPTH - 1:
                        continue
                    for pr in range(8):
                        pb = [PSB[4 + q] for q in range(4)]
                        y = B(Yt[:, pr, :], Ysb.d)
                        ysq = tmp()
                        ACT(ysq, y, AF.Square)
                        MM(pb[0], blk[:], y.ap, True, True, [dV, y.d])
                        MM(pb[1], blk[:], ysq.ap, True, True, [dV, ysq.d])
                        m = tmp()
                        ACT(m, pb[0], AF.Copy, scale=1.0 / 64)
                        msq = tmp()
                        TTo('pool', msq, m, m, ALU.mult)
                        var = tmp()
                        STT(var, pb[1], 1.0 / 64, msq, ALU.mult, ALU.subtract)
                        ACT(var, var, AF.Sqrt, bias=gneps[:, 0:1], extra=[dV])
                        S.op('dve', lambda E: E.reciprocal(out=var.ap, in_=var.ap), reads=[var.d], writes=[var.d])
                        yc = tmp()
                        TTo('pool', yc, y, m, ALU.subtract)
                        TTo('dve', yc, yc, var, ALU.mult)
                        ACT(yc, yc, AF.Identity, scale=vcol(12 + 6 * d + 4, pr), bias=vcol(12 + 6 * d + 5, pr), extra=[dV])
                        TTo('pool', yc, yc, bon[pr][0], ALU.add)
                        if d == 0:
                            ob, obd = o0r.next()
                            o = B(ob[:], obd)
                            TTo('dve', o, yc, gg[pr][0], ALU.mult)
                            S.dma('sp', o0T[pr, :, off:off + RT], o.ap, reads=[o.d])
                        else:
                            ob, obd = o0r.next()
                            o = B(ob[:], obd)
                            S.dma('sp', o.ap, o0T[pr, :, off:off + RT], writes=[o.d])
                            TTo('dve', yc, yc, gg[pr][0], ALU.mult)
                            ob2, ob2d = obr.next()
                            o2 = B(ob2[:], ob2d)
                            TTo('dve', o2, yc, o, ALU.add)
                            S.dma('sp', self.attnT[pr, :, off:off + RT], o2.ap, reads=[o2.d])
        S.barrier()

    def build(self):
        S = self.S
        self.setup()
        g = self.es_global
        self.epsc = self.sb(g, "epsc", [128, 1])
        S.op('dve', lambda E: E.memset(self.epsc[:], EPS), writes=[self.d_const])
        for st in self.plan:
            if st == 'init':
                self.stage_init()
            elif st == 'ada':
                self.stage_ada()
            elif st[0] == 'ffn':
                self.stage_ffn(st[1], st[2])
            elif st == 'final':
                self.stage_final()
            elif st[0] == 'dump':
                o = self.nc.dram_tensor("dbg_h%d" % st[1], [8, 128, T], F32, kind="ExternalOutput").ap()
                S.dma('sp', o, self.hT)
                S.barrier()
            elif st[0] == 'mix':
                l = st[1]
                kind = ('da', 'gq', 'rw')[l % 3]
                if kind == 'rw':
                    self.stage_rwkv(l)
                else:
                    self.stage_attn_proj(l, kind)
                    self.stage_attn_core(l, kind)
                self.stage_wo(l, kind)
            else:
                raise ValueError(st)
        self.dbg_out = {}
        for name in getattr(self, 'debug_dump', []):
            t = getattr(self, name)
            o = self.nc.dram_tensor("dbg_" + name, list(t.shape), t.dtype, kind="ExternalOutput").ap()
            S.dma('sp', o, t)
            self.dbg_out[name] = o
        S.finish()
        self.es_global.close()
        return self.nc


FULL_PLAN = ['init', 'ada']
for _l in range(DEPTH):
    FULL_PLAN += [('ffn', _l, 0), ('mix', _l), ('ffn', _l, 1)]
FULL_PLAN += ['final']


def rope_tables():
    f32 = np.float32
    rows = SEQ // 64
    row = np.repeat(np.arange(rows), 64).astype(f32)
    col = np.tile(np.arange(64), rows).astype(f32)
    out = {}
    for name, hd in (("da", 64), ("gq", 128)):
        nf = hd // 4
        inv = np.power(f32(10000.0), -np.arange(nf, dtype=f32) / f32(nf)).astype(f32)
        ang = np.concatenate([row[:, None] * inv, col[:, None] * inv], axis=-1).astype(f32)
        cos = np.cos(ang).astype(f32).T
        sin = np.sin(ang).astype(f32).T
        C = np.concatenate([cos, cos], axis=0)
        Sg = np.concatenate([-sin, sin], axis=0)
        reps = 128 // hd
        out["ropeC_" + name] = np.ascontiguousarray(np.tile(C, (reps, 1)))
        out["ropeS_" + name] = np.ascontiguousarray(np.tile(Sg, (reps, 1)))
    return out


def make_in_maps(inputs, ncores=8):
    f32 = np.float32
    ident = np.eye(128, dtype=f32)
    c_ctx = np.asarray(inputs["c_ctx"], f32)
    shared = {
        "ada_w": np.ascontiguousarray(inputs["ada_w"], f32),
        "ada_bT": np.ascontiguousarray(np.asarray(inputs["ada_b"], f32).reshape(DEPTH, 72, 128).transpose(2, 0, 1)),
        "ffn_wg": np.ascontiguousarray(inputs["ffn_wg"], f32),
        "ffn_wu": np.ascontiguousarray(inputs["ffn_wu"], f32),
        "ffn_wd": np.ascontiguousarray(inputs["ffn_wd"], f32),
        "final_gT": np.ascontiguousarray(np.asarray(inputs["final_g"], f32).reshape(8, 128).T),
        "ident": ident,
        "da_wqkv": np.ascontiguousarray(inputs["da_wqkv"], f32),
        "da_wo": np.ascontiguousarray(inputs["da_wo"], f32),
        "da_lamB": np.ascontiguousarray(np.broadcast_to(np.asarray(inputs["da_lam"], f32).reshape(2, 1, 256), (2, 128, 256))),
        "da_subln": np.ascontiguousarray(inputs["da_subln"], f32),
        "gq_wqkv": np.ascontiguousarray(inputs["gq_wqkv"], f32),
        "gq_wo": np.ascontiguousarray(inputs["gq_wo"], f32),
        "rw_wo": np.ascontiguousarray(inputs["rw_wo"], f32),
    }
    qg = np.asarray(inputs["gq_qk_g"], f32)[0]
    sw = np.concatenate([np.arange(64, 128), np.arange(0, 64)])
    shared["gq_gT"] = np.ascontiguousarray(np.stack([qg[0], qg[0][sw], qg[1], qg[1][sw]], axis=1))
    shared.update(rope_tables())
    for nm in ("rw_wr", "rw_wk", "rw_wv", "rw_w1", "rw_w2", "rw_a1", "rw_a2", "rw_g1", "rw_g2"):
        shared[nm] = np.ascontiguousarray(inputs[nm], f32)

    def fm(v):
        return np.asarray(v, f32).reshape(8, 128).T
    vecs = [fm(inputs["rw_mu"][0, a, i]) for a in range(2) for i in range(6)]
    for dd in range(2):
        for nm in ("rw_w0", "rw_a0", "rw_kk", "rw_ka", "rw_ln_g", "rw_ln_b"):
            vecs.append(fm(inputs[nm][0, dd]))
    for dd in range(2):
        vecs.append(fm(np.asarray(inputs["rw_rk"], f32)[0, dd].reshape(-1)))
    shared["rw_vec"] = np.ascontiguousarray(np.stack(vecs, axis=1))
    s_ = (np.arange(128) % 64)[:, None]
    t_ = (np.arange(256) % 64)[None, :]
    shared["rw_cst"] = np.ascontiguousarray(np.stack([(s_ < t_), (s_ <= t_), (s_ > t_), (s_ >= t_), (s_ == t_)], axis=1).astype(f32))
    shared["rw_rmask"] = np.ascontiguousarray(np.broadcast_to(((np.arange(256) % 64) != 0).astype(f32)[None, :], (128, 256)))
    lm = []
    for m_ in (1, 2, 4, 8, 16, 32):
        lm.append(((s_ // (2 * m_)) == (t_ // (2 * m_))) & ((s_ // m_) != (t_ // m_)))
    shared["rw_lmask"] = np.ascontiguousarray(np.stack(lm, axis=1).astype(f32))
    pb_ = np.arange(128) // 64
    shared["rw_blk"] = np.ascontiguousarray((pb_[:, None] == pb_[None, :]).astype(f32))
    maps = []
    for b in range(ncores):
        cT = np.stack([np.asarray(inputs["c"][b], f32).reshape(8, 128).T, c_ctx.reshape(8, 128).T], axis=-1)
        m = dict(shared)
        m["x"] = np.ascontiguousarray(inputs["x"][b], f32)
        m["ctx"] = np.ascontiguousarray(inputs["ctx"][b], f32)
        m["cT"] = np.ascontiguousarray(cT)
        maps.append(m)
    return maps


def kernel(**inputs):
    mk = MK(FULL_PLAN)
    nc = mk.build()
    maps = make_in_maps(inputs)
    maps = [{k: v for k, v in m.items() if k in mk.inp} for m in maps]
    res = run_bass_kernel_spmd(nc, maps, core_ids=list(range(8)))
    return np.stack([res.results[b]["out"] for b in range(8)], axis=0)
```

```python
import math
from contextlib import ExitStack
import numpy as np
import ml_dtypes
import concourse.bass as bass
import concourse.mybir as mybir
from concourse.bass_utils import run_bass_kernel_spmd

F32 = mybir.dt.float32
BF16 = mybir.dt.bfloat16
AF = mybir.ActivationFunctionType
ALU = mybir.AluOpType
AX = mybir.AxisListType

ENGS = ('pe', 'dve', 'act', 'pool', 'sp')
DMAQ = ('sp', 'pool', 'act')

D = 1024
DFF = 2816
NCTX = 256
SEQ = 4096
T = NCTX + SEQ
DEPTH = 4
EPS = 1e-6
TILES = [(0, 256)] + [(256 + 512 * i, 512) for i in range(8)]


_UID = [0]


class Dep:
    __slots__ = ('w', 'r')

    def __init__(self):
        self.w = {}
        self.r = {}


class _Rec:
    def __getattr__(self, name):
        def f(*a, **k):
            self.call = (name, a, k)
            return self
        return f


class Sched:
    def __init__(self, nc, ring=6):
        self.nc = nc
        self.q = {e: [] for e in ENGS}
        self.semobj = {}
        self.cnt = {e: 0 for e in ENGS}
        self.waited = {e: {} for e in ENGS}
        for e in ENGS:
            self.semobj['c_' + e] = nc.alloc_semaphore('c_' + e)
        self.K = ring
        self.dcount = {e: 0 for e in DMAQ}
        self.ringlast = {}
        for e in DMAQ:
            for i in range(ring):
                sid = f'd_{e}_{i}'
                self.semobj[sid] = nc.alloc_semaphore(sid)
                self.ringlast[sid] = 0
        self.ninstr = 0

    def _wait(self, eng, sid, val):
        if val <= 0 or self.waited[eng].get(sid, 0) >= val:
            return
        self.waited[eng][sid] = val
        sem = self.semobj[sid]
        self.q[eng].append(lambda E, sem=sem, val=val: E.wait_ge(sem, val))

    def _deps(self, eng, reads, writes):
        deps = {}
        for d in reads:
            for k, v in d.w.items():
                if deps.get(k, 0) < v:
                    deps[k] = v
        for d in writes:
            for k, v in d.w.items():
                if deps.get(k, 0) < v:
                    deps[k] = v
            for k, v in d.r.items():
                if deps.get(k, 0) < v:
                    deps[k] = v
        for k, v in deps.items():
            if eng == 'pe' and k == 'c_pe':
                continue
            self._wait(eng, k, v)

    def _mark(self, sid, v, reads, writes):
        for d in reads:
            if d.r.get(sid, 0) < v:
                d.r[sid] = v
        for d in writes:
            d.w = {sid: v}
            d.r = {}

    def op(self, eng, fn, reads=(), writes=()):
        self._deps(eng, reads, writes)
        self.cnt[eng] += 1
        v = self.cnt[eng]
        sem = self.semobj['c_' + eng]
        rec = _Rec()
        fn(rec)
        name, a, k = rec.call
        self.q[eng].append(lambda E, name=name, a=a, k=k, sem=sem: getattr(E, name)(*a, **k).then_inc(sem, 1))
        self._mark('c_' + eng, v, reads, writes)
        self.ninstr += 1

    def dma(self, queue, out, in_, reads=(), writes=(), **kw):
        self._deps(queue, reads, writes)
        i = self.dcount[queue]
        self.dcount[queue] += 1
        sid = f'd_{queue}_{i % self.K}'
        base = 16 * (i // self.K)
        self._wait(queue, sid, base)
        sem = self.semobj[sid]
        self.q[queue].append(
            lambda E, out=out, in_=in_, sem=sem, kw=kw: E.dma_start(out=out, in_=in_, **kw).then_inc(sem, 16))
        self.ringlast[sid] = base + 16
        self._mark(sid, base + 16, reads, writes)
        self.ninstr += 1

    def barrier(self):
        for e in ENGS:
            for f in ENGS:
                if f != e:
                    self._wait(e, 'c_' + f, self.cnt[f])
            for sid, v in self.ringlast.items():
                self._wait(e, sid, v)

    def finish(self):
        self.barrier()
        nc = self.nc
        q = self.q
        with nc.Block() as blk:
            @blk.sync
            def _(E):
                for f in q['sp']:
                    f(E)

            @blk.tensor
            def _(E):
                for f in q['pe']:
                    f(E)

            @blk.vector
            def _(E):
                for f in q['dve']:
                    f(E)

            @blk.scalar
            def _(E):
                for f in q['act']:
                    f(E)

            @blk.gpsimd
            def _(E):
                for f in q['pool']:
                    f(E)


class Ring:
    def __init__(self, nc, es, name, shape, dtype, n):
        _UID[0] += 1
        self.bufs = [es.enter_context(nc.sbuf_tensor(f'{name}{i}_{_UID[0]}', shape, dtype)) for i in range(n)]
        self.deps = [Dep() for _ in range(n)]
        self.i = 0

    def next(self):
        j = self.i % len(self.bufs)
        self.i += 1
        return self.bufs[j], self.deps[j]


class MK:
    def __init__(self, plan):
        self.plan = plan
        nc = bass.Bass("TRN2", target_bir_lowering=False)
        self.nc = nc
        self.S = Sched(nc)
        self.inp = {}
        self.es_global = ExitStack()

    def din(self, name, shape, dtype=F32):
        t = self.nc.dram_tensor(name, list(shape), dtype, kind="ExternalInput").ap()
        self.inp[name] = t
        return t

    def sb(self, es, name, shape, dtype=F32):
        _UID[0] += 1
        return es.enter_context(self.nc.sbuf_tensor(f"{name}_{_UID[0]}", list(shape), dtype))

    def setup(self):
        nc, S = self.nc, self.S
        g = self.es_global
        self.x = self.din("x", [SEQ, D])
        self.ctx = self.din("ctx", [NCTX, D])
        self.cT_in = self.din("cT", [128, 8, 2])
        self.ada_w = self.din("ada_w", [DEPTH, D, 9 * D])
        self.ada_bT = self.din("ada_bT", [128, DEPTH, 72])
        self.ffn_wg = self.din("ffn_wg", [DEPTH, 2, D, DFF])
        self.ffn_wu = self.din("ffn_wu", [DEPTH, 2, D, DFF])
        self.ffn_wd = self.din("ffn_wd", [DEPTH, 2, DFF, D])
        self.final_gT = self.din("final_gT", [128, 8])
        self.ident_in = self.din("ident", [128, 128])
        self.out = nc.dram_tensor("out", [SEQ, D], F32, kind="ExternalOutput").ap()
        self.da_wqkv = self.din("da_wqkv", [2, D, 3 * D])
        self.da_wo = self.din("da_wo", [2, D, D])
        self.da_lamB = self.din("da_lamB", [2, 128, 256])
        self.da_subln = self.din("da_subln", [2, 128])
        self.gq_wqkv = self.din("gq_wqkv", [1, D, 1536])
        self.gq_wo = self.din("gq_wo", [1, D, D])
        self.gq_gT = self.din("gq_gT", [128, 4])
        self.ropeC_da = self.din("ropeC_da", [128, SEQ])
        self.ropeS_da = self.din("ropeS_da", [128, SEQ])
        self.ropeC_gq = self.din("ropeC_gq", [128, SEQ])
        self.ropeS_gq = self.din("ropeS_gq", [128, SEQ])
        self.rw_wo = self.din("rw_wo", [1, D, D])
        self.rw_wr = self.din("rw_wr", [1, D, D])
        self.rw_wk = self.din("rw_wk", [1, D, D])
        self.rw_wv = self.din("rw_wv", [1, D, D])
        self.rw_w1 = self.din("rw_w1", [1, 2, D, 64])
        self.rw_w2 = self.din("rw_w2", [1, 2, 64, D])
        self.rw_a1 = self.din("rw_a1", [1, 2, D, 64])
        self.rw_a2 = self.din("rw_a2", [1, 2, 64, D])
        self.rw_g1 = self.din("rw_g1", [1, 2, D, 128])
        self.rw_g2 = self.din("rw_g2", [1, 2, 128, D])
        self.rw_vec = self.din("rw_vec", [128, 26, 8])
        self.rw_cst = self.din("rw_cst", [128, 5, 256])
        self.rw_rmask = self.din("rw_rmask", [128, 256])
        self.rw_blk = self.din("rw_blk", [128, 128])
        self.rw_lmask = self.din("rw_lmask", [128, 6, 256])
        self.qkT = nc.dram_tensor("qkT", [16, 128, T], BF16).ap()
        self.vS = nc.dram_tensor("vS", [T, D], BF16).ap()
        self.attnT = nc.dram_tensor("attnT", [8, 128, T], BF16).ap()
        self.hT = nc.dram_tensor("hT", [8, 128, T], F32).ap()
        self.hTv = self.hT.rearrange("c p t -> p c t")
        self.ident = self.sb(g, "ident_sb", [128, 128])
        self.identb = self.sb(g, "identb_sb", [128, 128], BF16)
        self.onesb = self.sb(g, "onesb", [128, 128], BF16)
        self.modT = self.sb(g, "modT", [128, DEPTH * 2, 72])
        self.fgT = self.sb(g, "fgT", [128, 8])
        self.mx = self.sb(g, "mx", [128, 32, 9])
        self.sel = self.sb(g, "sel", [128, 2, 128], BF16)
        self.d_mx = Dep()
        self.d_const = Dep()
        self.d_mod = Dep()
        S.dma('sp', self.ident[:], self.ident_in, writes=[self.d_const])
        S.dma('sp', self.fgT[:], self.final_gT, writes=[self.d_const])
        S.op('dve', lambda E: E.tensor_copy(out=self.identb[:], in_=self.ident[:]), reads=[self.d_const], writes=[self.d_const])
        S.op('dve', lambda E: E.memset(self.onesb[:], 1.0), writes=[self.d_const])
        S.op('dve', lambda E: E.memset(self.sel[:], 0.0), writes=[self.d_const])
        S.op('dve', lambda E: E.memset(self.sel[0:64, 0, :], 1.0), writes=[self.d_const])
        S.op('dve', lambda E: E.memset(self.sel[64:128, 1, :], 1.0), writes=[self.d_const])
        self.ps = [g.enter_context(nc.psum_tensor(f"ps{i}", [128, 512], F32)) for i in range(8)]
        self.dps = [Dep() for _ in range(8)]

    def mod(self, l, w, m, c):
        j = m * 8 + c
        return self.modT[:, l * 2 + w, j:j + 1]

    def stage_init(self):
        nc, S = self.nc, self.S
        with ExitStack() as es:
            xin = Ring(nc, es, "xin", [128, 4, D], F32, 2)
            hsb = Ring(nc, es, "hsb", [128, 8, 512], F32, 2)
            for (off, tt) in TILES:
                ns = tt // 128
                xb, xd = xin.next()
                src = self.ctx if off == 0 else self.x[off - NCTX: off - NCTX + tt, :]
                S.dma('sp', xb[:, 0:ns, :], src.rearrange("(s p) f -> p s f", p=128), writes=[xd])
                hb, hd = hsb.next()
                for c in range(8):
                    for s in range(ns):
                        S.op('pe', lambda E, c=c, s=s, xb=xb: E.transpose(
                            out=self.ps[c][:, s * 128:(s + 1) * 128], in_=xb[:, s, c * 128:(c + 1) * 128], identity=self.ident[:]),
                            reads=[xd, self.d_const], writes=[self.dps[c]])
                    eng = 'dve' if c % 2 == 0 else 'act'
                    if eng == 'dve':
                        S.op('dve', lambda E, c=c, hb=hb, tt=tt: E.tensor_copy(out=hb[:, c, 0:tt], in_=self.ps[c][:, 0:tt]),
                             reads=[self.dps[c]], writes=[hd])
                    else:
                        S.op('act', lambda E, c=c, hb=hb, tt=tt: E.activation(out=hb[:, c, 0:tt], in_=self.ps[c][:, 0:tt], func=AF.Copy),
                             reads=[self.dps[c]], writes=[hd])
                S.dma('pool', self.hTv[:, :, off:off + tt], hb[:, :, 0:tt], reads=[hd])
        S.barrier()

    def stage_ada(self):
        nc, S = self.nc, self.S
        with ExitStack() as es:
            cT = self.sb(es, "cT_sb", [128, 8, 2])
            cact = self.sb(es, "cact", [128, 8, 2])
            abT = self.sb(es, "abT", [128, DEPTH, 72])
            dc = Dep()
            S.dma('sp', cT[:], self.cT_in, writes=[dc])
            S.dma('sp', abT[:], self.ada_bT, writes=[dc])
            S.op('act', lambda E: E.activation(out=cact[:], in_=cT[:], func=AF.Silu), reads=[dc], writes=[dc])
            wr = Ring(nc, es, "adaw", [128, 8, 1152], F32, 2)
            for l in range(DEPTH):
                wv = self.ada_w[l].rearrange("(k p) n -> p k n", p=128)
                bank = l % 2
                for piece in range(8):
                    wb, wd = wr.next()
                    S.dma('sp' if piece % 2 == 0 else 'pool', wb[:], wv[:, :, piece * 1152:(piece + 1) * 1152], writes=[wd])
                    for jj in range(9):
                        j = piece * 9 + jj
                        for k in range(8):
                            S.op('pe', lambda E, wb=wb, jj=jj, k=k, j=j, bank=bank: E.matmul(
                                self.ps[bank][:, 2 * j:2 * j + 2], lhsT=wb[:, k, jj * 128:(jj + 1) * 128], rhs=cact[:, k, :],
                                start=(k == 0), stop=(k == 7)), reads=[wd, dc], writes=[self.dps[bank]])
                for w in range(2):
                    S.op('dve', lambda E, l=l, w=w, bank=bank: E.tensor_tensor(
                        out=self.modT[:, l * 2 + w, :], in0=self.ps[bank][:, w:144:2], in1=abT[:, l, :], op=ALU.add),
                        reads=[self.dps[bank], dc], writes=[self.d_mod])
            for m in (1, 4, 7):
                S.op('dve', lambda E, m=m: E.tensor_scalar(out=self.modT[:, :, m * 8:m * 8 + 8], in0=self.modT[:, :, m * 8:m * 8 + 8],
                                                           scalar1=1.0, scalar2=None, op0=ALU.add),
                     reads=[self.d_mod], writes=[self.d_mod])
            for m in (2, 8):
                S.op('dve', lambda E, m=m: E.tensor_scalar(out=self.modT[:, :, m * 8:m * 8 + 8], in0=self.modT[:, :, m * 8:m * 8 + 8],
                                                           scalar1=0.5, scalar2=None, op0=ALU.mult),
                     reads=[self.d_mod], writes=[self.d_mod])
        S.barrier()

    def make_norm_bufs(self, es):
        nc = self.nc
        nb = {}
        nb['hin'] = Ring(nc, es, "hin", [128, 8, 512], F32, 1)
        nb['sq'] = Ring(nc, es, "sq", [128, 512], BF16, 2)
        nb['tmp'] = Ring(nc, es, "ntmp", [128, 512], F32, 2)
        nb['rs'] = Ring(nc, es, "rs", [128, 512], F32, 2)
        return nb

    def norm_mod(self, nb, off, tt, l, mshift, uT, ud, ps_stat=7, out_f32=False):
        S = self.S
        w = 1 if off == 0 else 0
        hb, hd = nb['hin'].next()
        S.dma('sp', hb[:, :, 0:tt], self.hTv[:, :, off:off + tt], writes=[hd])
        pst, dpst = self.ps[ps_stat], self.dps[ps_stat]
        for c in range(8):
            sq, sqd = nb['sq'].next()
            S.op('act', lambda E, sq=sq, hb=hb, c=c: E.activation(out=sq[:, 0:tt], in_=hb[:, c, 0:tt], func=AF.Square),
                 reads=[hd], writes=[sqd])
            S.op('pe', lambda E, sq=sq, c=c: E.matmul(pst[:, 0:tt], lhsT=self.onesb[:], rhs=sq[:, 0:tt], start=(c == 0), stop=(c == 7)),
                 reads=[sqd, self.d_const], writes=[dpst])
        r1, r1d = nb['rs'].next()
        r2, r2d = nb['rs'].next()
        S.op('act', lambda E: E.activation(out=r1[:, 0:tt], in_=pst[:, 0:tt], func=AF.Sqrt, scale=1.0 / D, bias=self.epsc[:, 0:1]),
             reads=[dpst, self.d_const], writes=[r1d])
        S.op('dve', lambda E: E.reciprocal(out=r2[:, 0:tt], in_=r1[:, 0:tt]), reads=[r1d], writes=[r2d])
        for c in range(8):
            tmp, td = nb['tmp'].next()
            S.op('dve', lambda E, tmp=tmp, c=c: E.tensor_tensor(out=tmp[:, 0:tt], in0=hb[:, c, 0:tt], in1=r2[:, 0:tt], op=ALU.mult),
                 reads=[hd, r2d], writes=[td])
            S.op('act', lambda E, tmp=tmp, c=c: E.activation(out=uT[:, c, 0:tt], in_=tmp[:, 0:tt], func=AF.Identity,
                                                             scale=self.mod(l, w, mshift + 1, c), bias=self.mod(l, w, mshift, c)),
                 reads=[td, self.d_mod], writes=[ud])

    def stage_ffn(self, l, s):
        nc, S = self.nc, self.S
        mb = 0 if s == 0 else 6
        with ExitStack() as es:
            wg = self.sb(es, "wg", [128, 8, DFF], BF16)
            wu = self.sb(es, "wu", [128, 8, DFF], BF16)
            wd = self.sb(es, "wd", [128, 22, D], BF16)
            dwg, dwu, dwd = Dep(), Dep(), Dep()
            gv = self.ffn_wg[l, s].rearrange("(k p) n -> p k n", p=128)
            uv = self.ffn_wu[l, s].rearrange("(k p) n -> p k n", p=128)
            dv = self.ffn_wd[l, s].rearrange("(j p) n -> p j n", p=128)
            for k in range(8):
                S.dma('pool', wg[:, k, :], gv[:, k, :], writes=[dwg])
                S.dma('pool', wu[:, k, :], uv[:, k, :], writes=[dwu])
            for j in range(0, 22, 2):
                S.dma('pool', wd[:, j:j + 2, :], dv[:, j:j + 2, :], writes=[dwd])
            nb = self.make_norm_bufs(es)
            uTr = Ring(nc, es, "uT", [128, 8, 512], BF16, 2)
            sgr = Ring(nc, es, "sg", [128, 512], BF16, 2)
            aT = self.sb(es, "aT", [128, 22, 512], BF16)
            daT = Dep()
            hres = Ring(nc, es, "hres", [128, 512], F32, 2)

            def prologue(i):
                off, tt = TILES[i]
                u, ud = uTr.next()
                self.norm_mod(nb, off, tt, l, mb, u, ud)
                return u, ud

            cur = prologue(0)
            for i, (off, tt) in enumerate(TILES):
                w = 1 if off == 0 else 0
                u, ud = cur
                for j in range(22):
                    bg, bu = (0, 1) if j % 2 == 0 else (2, 3)
                    for k in range(8):
                        S.op('pe', lambda E, j=j, k=k, bg=bg, u=u: E.matmul(self.ps[bg][:, 0:tt], lhsT=wg[:, k, j * 128:(j + 1) * 128],
                                                                        rhs=u[:, k, 0:tt], start=(k == 0), stop=(k == 7)),
                             reads=[dwg, ud], writes=[self.dps[bg]])
                    for k in range(8):
                        S.op('pe', lambda E, j=j, k=k, bu=bu, u=u: E.matmul(self.ps[bu][:, 0:tt], lhsT=wu[:, k, j * 128:(j + 1) * 128],
                                                                        rhs=u[:, k, 0:tt], start=(k == 0), stop=(k == 7)),
                             reads=[dwu, ud], writes=[self.dps[bu]])
                    sg, sgd = sgr.next()
                    S.op('act', lambda E, sg=sg, bg=bg: E.activation(out=sg[:, 0:tt], in_=self.ps[bg][:, 0:tt], func=AF.Silu),
                         reads=[self.dps[bg]], writes=[sgd])
                    S.op('dve', lambda E, sg=sg, bu=bu, j=j: E.tensor_tensor(out=aT[:, j, 0:tt], in0=self.ps[bu][:, 0:tt], in1=sg[:, 0:tt], op=ALU.mult),
                         reads=[self.dps[bu], sgd], writes=[daT])
                    if j == 11 and i + 1 < len(TILES):
                        cur = prologue(i + 1)
                for c in range(8):
                    by = 4 + (c % 2)
                    hr, hrd = hres.next()
                    S.dma('sp', hr[:, 0:tt], self.hT[c, :, off:off + tt], writes=[hrd])
                    for j in range(22):
                        S.op('pe', lambda E, j=j, c=c, by=by: E.matmul(self.ps[by][:, 0:tt], lhsT=wd[:, j, c * 128:(c + 1) * 128],
                                                                    rhs=aT[:, j, 0:tt], start=(j == 0), stop=(j == 21)),
                             reads=[dwd, daT], writes=[self.dps[by]])
                    S.op('dve', lambda E, hr=hr, c=c, by=by, w=w: E.scalar_tensor_tensor(
                        out=hr[:, 0:tt], in0=self.ps[by][:, 0:tt], scalar=self.mod(l, w, mb + 2, c), in1=hr[:, 0:tt], op0=ALU.mult, op1=ALU.add),
                        reads=[self.dps[by], hrd, self.d_mod], writes=[hrd])
                    S.dma('pool', self.hT[c, :, off:off + tt], hr[:, 0:tt], reads=[hrd])
        S.barrier()

    def stage_final(self):
        nc, S = self.nc, self.S
        with ExitStack() as es:
            hin = Ring(nc, es, "fhin", [128, 8, 512], F32, 2)
            sqr = Ring(nc, es, "fsq", [128, 512], BF16, 2)
            rs = Ring(nc, es, "frs", [128, 512], F32, 2)
            xn = Ring(nc, es, "fxn", [128, 512], F32, 3)
            osb = Ring(nc, es, "fosb", [128, 4, D], F32, 2)
            for (off, tt) in TILES[1:]:
                hb, hd = hin.next()
                S.dma('sp', hb[:], self.hTv[:, :, off:off + tt], writes=[hd])
                for c in range(8):
                    sq, sqd = sqr.next()
                    S.op('act', lambda E, sq=sq, hb=hb, c=c: E.activation(out=sq[:], in_=hb[:, c, :], func=AF.Square), reads=[hd], writes=[sqd])
                    S.op('pe', lambda E, sq=sq, c=c: E.matmul(self.ps[7][:], lhsT=self.onesb[:], rhs=sq[:], start=(c == 0), stop=(c == 7)),
                         reads=[sqd, self.d_const], writes=[self.dps[7]])
                r1, r1d = rs.next()
                r2, r2d = rs.next()
                S.op('act', lambda E, r1=r1: E.activation(out=r1[:], in_=self.ps[7][:], func=AF.Sqrt, scale=1.0 / D, bias=self.epsc[:, 0:1]),
                     reads=[self.dps[7], self.d_const], writes=[r1d])
                S.op('dve', lambda E, r1=r1, r2=r2: E.reciprocal(out=r2[:], in_=r1[:]), reads=[r1d], writes=[r2d])
                ob, od = osb.next()
                for c in range(8):
                    x, xd = xn.next()
                    S.op('dve', lambda E, x=x, hb=hb, c=c, r2=r2: E.scalar_tensor_tensor(
                        out=x[:], in0=hb[:, c, :], scalar=self.fgT[:, c:c + 1], in1=r2[:], op0=ALU.mult, op1=ALU.mult),
                        reads=[hd, r2d, self.d_const], writes=[xd])
                    b = c % 4
                    for s in range(4):
                        S.op('pe', lambda E, x=x, s=s, b=b: E.transpose(out=self.ps[b][:, s * 128:(s + 1) * 128], in_=x[:, s * 128:(s + 1) * 128],
                                                                     identity=self.ident[:]),
                             reads=[xd, self.d_const], writes=[self.dps[b]])
                    if c % 2 == 0:
                        S.op('act', lambda E, ob=ob, b=b, c=c: E.activation(
                            out=ob[:, :, c * 128:(c + 1) * 128], in_=self.ps[b][:].rearrange("p (s f) -> p s f", s=4), func=AF.Copy),
                            reads=[self.dps[b]], writes=[od])
                    else:
                        S.op('dve', lambda E, ob=ob, b=b, c=c: E.tensor_copy(
                            out=ob[:, :, c * 128:(c + 1) * 128], in_=self.ps[b][:].rearrange("p (s f) -> p s f", s=4)),
                            reads=[self.dps[b]], writes=[od])
                S.dma('pool', self.out[off - NCTX:off - NCTX + tt, :].rearrange("(s p) f -> p s f", p=128), ob[:], reads=[od])
        S.barrier()

    def stage_attn_proj(self, l, kind):
        nc, S = self.nc, self.S
        j = l // 3
        da = (kind == 'da')
        nqk = 16 if da else 10
        nqkc = nqk * 128
        nvc = 1024 if da else 256
        wsrc = (self.da_wqkv[j] if da else self.gq_wqkv[0]).rearrange("(k p) n -> p k n", p=128)
        ncols = nqkc + nvc
        with ExitStack() as es:
            w = self.sb(es, "aw", [128, 8, ncols], BF16)
            wsw = self.sb(es, "awsw", [128, 8, nqkc], BF16)
            Ct = self.sb(es, "ropeC", [128, SEQ])
            St = self.sb(es, "ropeS", [128, SEQ])
            dw, dtab = Dep(), Dep()
            for k in range(8):
                S.dma('pool', w[:, k, :], wsrc[:, k, :], writes=[dw])
            S.dma('sp', Ct[:], self.ropeC_da if da else self.ropeC_gq, writes=[dtab])
            S.dma('sp', St[:], self.ropeS_da if da else self.ropeS_gq, writes=[dtab])
            hb = 32 if da else 64
            for k in range(8):
                src4 = w[:, k, 0:nqkc].rearrange("p (b h d) -> p b h d", h=2, d=hb)
                dst4 = wsw[:, k, :].rearrange("p (b h d) -> p b h d", h=2, d=hb)
                S.op('dve', lambda E, src4=src4, dst4=dst4: E.tensor_copy(out=dst4[:, :, 0, :], in_=src4[:, :, 1, :]), reads=[dw], writes=[dw])
                S.op('dve', lambda E, src4=src4, dst4=dst4: E.tensor_copy(out=dst4[:, :, 1, :], in_=src4[:, :, 0, :]), reads=[dw], writes=[dw])
            if not da:
                gT = self.sb(es, "gqg", [128, 4])
                S.dma('sp', gT[:], self.gq_gT, writes=[dtab])
            nb = self.make_norm_bufs(es)
            uTr = Ring(nc, es, "auT", [128, 8, 512], BF16, 2)
            t1r = Ring(nc, es, "t1", [128, 512], F32, 2)
            t2r = Ring(nc, es, "t2", [128, 512], F32, 2)
            sqr = Ring(nc, es, "asq", [128, 512], BF16, 2)
            rsr = Ring(nc, es, "ars", [128, 512], F32, 2)
            qor = Ring(nc, es, "qo", [128, 16, 512], BF16, 1)
            vsr = Ring(nc, es, "vsb", [128, 1024], BF16, 2)
            S.op('dve', lambda E: E.memset(self.mx[:], 0.0), writes=[self.d_mx])
            for ti, (off, tt) in enumerate(TILES):
                lat = off != 0
                r0 = off - NCTX
                u, ud = uTr.next()
                self.norm_mod(nb, off, tt, l, 3, u, ud)
                qo, qod = qor.next()
                for ch in range(nqk):
                    bA, bB = (0, 2) if ch % 2 == 0 else (1, 3)
                    cols = slice(ch * 128, (ch + 1) * 128)
                    for k in range(8):
                        S.op('pe', lambda E, k=k, bA=bA, u=u, cols=cols: E.matmul(self.ps[bA][:, 0:tt], lhsT=w[:, k, cols], rhs=u[:, k, 0:tt],
                                                                          start=(k == 0), stop=(k == 7)), reads=[dw, ud], writes=[self.dps[bA]])
                    if lat:
                        for k in range(8):
                            S.op('pe', lambda E, k=k, bB=bB, u=u, cols=cols: E.matmul(self.ps[bB][:, 0:tt], lhsT=wsw[:, k, cols], rhs=u[:, k, 0:tt],
                                                                              start=(k == 0), stop=(k == 7)), reads=[dw, ud], writes=[self.dps[bB]])
                    t1, t1d = t1r.next()
                    t2, t2d = t2r.next()
                    if da:
                        if lat:
                            S.op('dve', lambda E, t1=t1, bA=bA: E.tensor_tensor(out=t1[:, 0:tt], in0=self.ps[bA][:, 0:tt], in1=Ct[:, r0:r0 + tt], op=ALU.mult),
                                 reads=[self.dps[bA], dtab], writes=[t1d])
                            S.op('dve', lambda E, t2=t2, bB=bB: E.tensor_tensor(out=t2[:, 0:tt], in0=self.ps[bB][:, 0:tt], in1=St[:, r0:r0 + tt], op=ALU.mult),
                                 reads=[self.dps[bB], dtab], writes=[t2d])
                            S.op('pool', lambda E, t1=t1, t2=t2, qo=qo, ch=ch: E.tensor_tensor(out=qo[:, ch, 0:tt], in0=t1[:, 0:tt], in1=t2[:, 0:tt], op=ALU.add),
                                 reads=[t1d, t2d], writes=[qod])
                        else:
                            S.op('act', lambda E, qo=qo, ch=ch, bA=bA: E.activation(out=qo[:, ch, 0:tt], in_=self.ps[bA][:, 0:tt], func=AF.Copy),
                                 reads=[self.dps[bA]], writes=[qod])
                    else:
                        gi = 0 if ch < 8 else 2
                        sq, sqd = sqr.next()
                        S.op('act', lambda E, sq=sq, bA=bA: E.activation(out=sq[:, 0:tt], in_=self.ps[bA][:, 0:tt], func=AF.Square),
                             reads=[self.dps[bA]], writes=[sqd])
                        S.op('pe', lambda E, sq=sq: E.matmul(self.ps[4][:, 0:tt], lhsT=self.onesb[:], rhs=sq[:, 0:tt], start=True, stop=True),
                             reads=[sqd, self.d_const], writes=[self.dps[4]])
                        r1, r1d = rsr.next()
                        r2, r2d = rsr.next()
                        S.op('act', lambda E, r1=r1: E.activation(out=r1[:, 0:tt], in_=self.ps[4][:, 0:tt], func=AF.Sqrt, scale=1.0 / 128, bias=self.epsc[:, 0:1]),
                             reads=[self.dps[4], self.d_const], writes=[r1d])
                        S.op('dve', lambda E, r1=r1, r2=r2: E.reciprocal(out=r2[:, 0:tt], in_=r1[:, 0:tt]), reads=[r1d], writes=[r2d])
                        S.op('dve', lambda E, t1=t1, bA=bA, r2=r2: E.tensor_tensor(out=t1[:, 0:tt], in0=self.ps[bA][:, 0:tt], in1=r2[:, 0:tt], op=ALU.mult),
                             reads=[self.dps[bA], r2d], writes=[t1d])
                        if lat:
                            S.op('dve', lambda E, t1=t1, gi=gi: E.scalar_tensor_tensor(out=t1[:, 0:tt], in0=t1[:, 0:tt], scalar=gT[:, gi:gi + 1], in1=Ct[:, r0:r0 + tt],
                                                                                 op0=ALU.mult, op1=ALU.mult), reads=[t1d, dtab], writes=[t1d])
                            S.op('dve', lambda E, t2=t2, bB=bB, r2=r2: E.tensor_tensor(out=t2[:, 0:tt], in0=self.ps[bB][:, 0:tt], in1=r2[:, 0:tt], op=ALU.mult),
                                 reads=[self.dps[bB], r2d], writes=[t2d])
                            S.op('dve', lambda E, t2=t2, gi=gi: E.scalar_tensor_tensor(out=t2[:, 0:tt], in0=t2[:, 0:tt], scalar=gT[:, gi + 1:gi + 2], in1=St[:, r0:r0 + tt],
                                                                                 op0=ALU.mult, op1=ALU.mult), reads=[t2d, dtab], writes=[t2d])
                            S.op('pool', lambda E, t1=t1, t2=t2, qo=qo, ch=ch: E.tensor_tensor(out=qo[:, ch, 0:tt], in0=t1[:, 0:tt], in1=t2[:, 0:tt], op=ALU.add),
                                 reads=[t1d, t2d], writes=[qod])
                        else:
                            S.op('dve', lambda E, t1=t1, gi=gi, qo=qo, ch=ch: E.tensor_scalar(out=qo[:, ch, 0:tt], in0=t1[:, 0:tt], scalar1=gT[:, gi:gi + 1], scalar2=None,
                                                                                      op0=ALU.mult), reads=[t1d, dtab], writes=[qod])
                    sq2, sq2d = sqr.next()
                    S.op('act', lambda E, sq2=sq2, qo=qo, ch=ch: E.activation(out=sq2[:, 0:tt], in_=qo[:, ch, 0:tt], func=AF.Square), reads=[qod], writes=[sq2d])
                    for c in range(2 if da else 1):
                        bM = 5 + c
                        lhs = self.sel[:, c, :] if da else self.onesb[:]
                        S.op('pe', lambda E, sq2=sq2, bM=bM, lhs=lhs: E.matmul(self.ps[bM][:, 0:tt], lhsT=lhs, rhs=sq2[:, 0:tt], start=True, stop=True),
                             reads=[sq2d, self.d_const], writes=[self.dps[bM]])
                        idx = ch * 2 + c if da else ch
                        S.op('dve', lambda E, bM=bM, idx=idx, ti=ti: E.tensor_reduce(out=self.mx[:, idx, ti:ti + 1], in_=self.ps[bM][:, 0:tt], axis=AX.X, op=ALU.max),
                             reads=[self.dps[bM]], writes=[self.d_mx])
                S.dma('sp', self.qkT[0:nqk].rearrange("c p t -> p c t")[:, :, off:off + tt], qo[:, 0:nqk, 0:tt], reads=[qod])
                for s in range(tt // 128):
                    vb, vd = vsr.next()
                    for hv in range(nvc // 512 if nvc >= 512 else 1):
                        wv = min(512, nvc)
                        b = 0 + hv
                        for k in range(8):
                            S.op('pe', lambda E, k=k, b=b, u=u, s=s, hv=hv, wv=wv: E.matmul(
                                self.ps[b][:, 0:wv], lhsT=u[:, k, s * 128:(s + 1) * 128], rhs=w[:, k, nqkc + hv * 512: nqkc + hv * 512 + wv],
                                start=(k == 0), stop=(k == 7)), reads=[dw, ud], writes=[self.dps[b]])
                        if hv == 0:
                            S.op('act', lambda E, vb=vb, b=b, wv=wv: E.activation(out=vb[:, 0:wv], in_=self.ps[b][:, 0:wv], func=AF.Copy),
                                 reads=[self.dps[b]], writes=[vd])
                        else:
                            S.op('dve', lambda E, vb=vb, b=b, hv=hv: E.tensor_copy(out=vb[:, 512:1024], in_=self.ps[b][:, 0:512]),
                                 reads=[self.dps[b]], writes=[vd])
                    S.dma('sp', self.vS[off + s * 128: off + (s + 1) * 128, 0:nvc], vb[:, 0:nvc], reads=[vd])
        S.barrier()

    def stage_attn_core(self, l, kind):
        nc, S = self.nc, self.S
        j = l // 3
        da = (kind == 'da')
        ncomp = 2 if da else 1
        scale = (64 if da else 128) ** -0.5
        lam_init = 0.8 - 0.6 * math.exp(-0.3 * l)
        psTb = self.ps[6][:].bitcast(BF16)
        with ExitStack() as es:
            mq = self.sb(es, "mq", [128, 32])
            negm = self.sb(es, "negm", [128, 16])
            dm = Dep()
            S.op('dve', lambda E: E.tensor_reduce(out=mq[:], in_=self.mx[:], axis=AX.X, op=ALU.max), reads=[self.d_mx], writes=[dm])
            if da:
                S.op('dve', lambda E: E.tensor_tensor(out=negm[:], in0=mq[:, 0:16], in1=mq[:, 16:32], op=ALU.mult), reads=[dm], writes=[dm])
            else:
                for g in range(2):
                    S.op('dve', lambda E, g=g: E.tensor_scalar(out=negm[:, g * 4:(g + 1) * 4], in0=mq[:, g * 4:(g + 1) * 4], scalar1=mq[:, 8 + g:9 + g],
                                                               scalar2=None, op0=ALU.mult), reads=[dm], writes=[dm])
            S.op('act', lambda E: E.activation(out=negm[:], in_=negm[:], func=AF.Sqrt), reads=[dm], writes=[dm])
            S.op('dve', lambda E: E.tensor_scalar(out=negm[:], in0=negm[:], scalar1=-(scale * 1.02), scalar2=None, op0=ALU.mult), reads=[dm], writes=[dm])
            if da:
                lb = self.sb(es, "lamb", [128, 256])
                lt = self.sb(es, "lamt", [128, 128])
                lam = self.sb(es, "lam", [128, 4])
                S.dma('sp', lb[:], self.da_lamB[j], writes=[dm])
                S.op('dve', lambda E: E.tensor_tensor(out=lt[:, 0:64], in0=lb[:, 0:64], in1=lb[:, 64:128], op=ALU.mult), reads=[dm], writes=[dm])
                S.op('dve', lambda E: E.tensor_tensor(out=lt[:, 64:128], in0=lb[:, 128:192], in1=lb[:, 192:256], op=ALU.mult), reads=[dm], writes=[dm])
                S.op('dve', lambda E: E.tensor_reduce(out=lam[:, 0:2], in_=lt[:].rearrange("p (a d) -> p a d", a=2), axis=AX.X, op=ALU.add), reads=[dm], writes=[dm])
                S.op('act', lambda E: E.activation(out=lam[:, 0:2], in_=lam[:, 0:2], func=AF.Exp), reads=[dm], writes=[dm])
                S.op('dve', lambda E: E.tensor_tensor(out=lam[:, 2:3], in0=lam[:, 1:2], in1=lam[:, 0:1], op=ALU.subtract), reads=[dm], writes=[dm])
                S.op('dve', lambda E: E.tensor_scalar(out=lam[:, 3:4], in0=lam[:, 2:3], scalar1=-lam_init, scalar2=None, op0=ALU.add), reads=[dm], writes=[dm])
            KTr = Ring(nc, es, "KT", [128, T], BF16, 2)
            QTr = Ring(nc, es, "QT", [128, T], BF16, 2)
            Vr = Ring(nc, es, "Vh", [128, 34, 129], BF16, 2)
            for vb in Vr.bufs:
                S.op('pool', lambda E, vb=vb: E.memset(vb[:], 1.0), writes=Vr.deps)
            PTr = Ring(nc, es, "PT", [128, 512], BF16, 4)
            ocr = [Ring(nc, es, f"oc{c}", [128, 4, 129], F32, 2) for c in range(ncomp)]
            smr = Ring(nc, es, "sm", [128, 8], F32, 4)
            otr = Ring(nc, es, "ot", [128, 128], F32, 3)
            onr = Ring(nc, es, "on", [128, 128], BF16, 10)
            aor = Ring(nc, es, "ao", [128, 512], BF16, 2)
            pending = []
            for h in range(8):
                kch = (8 + h) if da else (8 + h // 4)
                vch = h if da else h // 4
                KT, KTd = KTr.next()
                QT, QTd = QTr.next()
                Vh, Vd = Vr.next()
                S.dma('sp', KT[:], self.qkT[kch], writes=[KTd])
                S.dma('sp', QT[:], self.qkT[h], writes=[QTd])
                S.dma('pool', Vh[:, :, 0:128], self.vS[:, vch * 128:(vch + 1) * 128].rearrange("(n p) e -> p n e", p=128), writes=[Vd])
                for (off, tt) in TILES:
                    ns = tt // 128
                    nkc = 2 if off == 0 else 34
                    if off == 0 and l == DEPTH - 1:
                        continue
                    ocs = []
                    for c in range(ncomp):
                        rows = slice(64 * c, 64 * c + 64) if da else slice(0, 128)
                        idx = h * 2 + c if da else h
                        SB = (0, 1, 7)

                        def emit_S(kc):
                            bS = SB[kc % 3]
                            S.op('pe', lambda E: E.matmul(
                                self.ps[bS][:, 0:tt], lhsT=KT[rows, kc * 128:(kc + 1) * 128], rhs=QT[rows, off:off + tt], start=True, stop=True),
                                reads=[KTd, QTd], writes=[self.dps[bS]])
                        for kc0 in range(min(3, nkc)):
                            emit_S(kc0)
                        for kc in range(nkc):
                            if pending and c == 0 and kc == min(3, nkc - 1):
                                pending.pop(0)()
                            bS = SB[kc % 3]
                            PT, PTd = PTr.next()
                            S.op('act', lambda E: E.activation(out=PT[:, 0:tt], in_=self.ps[bS][:, 0:tt], func=AF.Exp,
                                                               scale=scale, bias=negm[:, idx:idx + 1]),
                                 reads=[self.dps[bS], dm], writes=[PTd])
                            for s in range(ns):
                                S.op('pe', lambda E: E.matmul(
                                    self.ps[2 + s][:, 0:129], lhsT=PT[:, s * 128:(s + 1) * 128], rhs=Vh[:, kc, :], start=(kc == 0), stop=(kc == nkc - 1)),
                                    reads=[PTd, Vd], writes=[self.dps[2 + s]])
                            if kc + 3 < nkc:
                                emit_S(kc + 3)
                        oc, ocd = ocr[c].next()
                        for s in range(ns):
                            if s % 2 == 0:
                                S.op('act', lambda E, oc=oc, s=s: E.activation(out=oc[:, s, :], in_=self.ps[2 + s][:, 0:129], func=AF.Copy),
                                     reads=[self.dps[2 + s]], writes=[ocd])
                            else:
                                S.op('dve', lambda E, oc=oc, s=s: E.tensor_copy(out=oc[:, s, :], in_=self.ps[2 + s][:, 0:129]),
                                     reads=[self.dps[2 + s]], writes=[ocd])
                        ocs.append((oc, ocd))
                    ons = []
                    for s in range(ns):
                        sm, smd = smr.next()
                        on, ond = onr.next()
                        o0, o0d = ocs[0]
                        S.op('dve', lambda E, sm=sm, o0=o0, s=s: E.reciprocal(out=sm[:, 0:1], in_=o0[:, s, 128:129]), reads=[o0d], writes=[smd])
                        if da:
                            o1, o1d = ocs[1]
                            ot, otd = otr.next()
                            S.op('dve', lambda E, sm=sm, o1=o1, s=s: E.reciprocal(out=sm[:, 1:2], in_=o1[:, s, 128:129]), reads=[o1d, smd], writes=[smd])
                            S.op('dve', lambda E, sm=sm: E.tensor_tensor(out=sm[:, 2:3], in0=sm[:, 1:2], in1=lam[:, 3:4], op=ALU.mult), reads=[smd, dm], writes=[smd])
                            S.op('dve', lambda E, sm=sm, o0=o0, ot=ot, s=s: E.tensor_scalar(out=ot[:], in0=o0[:, s, 0:128], scalar1=sm[:, 0:1], scalar2=None, op0=ALU.mult),
                                 reads=[o0d, smd], writes=[otd])
                            S.op('dve', lambda E, sm=sm, o1=o1, ot=ot, s=s: E.scalar_tensor_tensor(out=ot[:], in0=o1[:, s, 0:128], scalar=sm[:, 2:3], in1=ot[:],
                                                                                             op0=ALU.mult, op1=ALU.add), reads=[o1d, smd, otd], writes=[otd])
                            ot2, ot2d = otr.next()
                            S.op('act', lambda E, ot=ot, ot2=ot2, sm=sm: E.activation(out=ot2[:], in_=ot[:], func=AF.Square, accum_out=sm[:, 3:4]),
                                 reads=[otd, smd], writes=[ot2d, smd])
                            S.op('act', lambda E, sm=sm: E.activation(out=sm[:, 4:5], in_=sm[:, 3:4], func=AF.Sqrt, scale=1.0 / 128, bias=self.epsc[:, 0:1]),
                                 reads=[smd, self.d_const], writes=[smd])
                            S.op('dve', lambda E, sm=sm: E.reciprocal(out=sm[:, 5:6], in_=sm[:, 4:5]), reads=[smd], writes=[smd])
                            S.op('dve', lambda E, sm=sm, ot=ot, on=on: E.tensor_scalar(out=on[:], in0=ot[:], scalar1=sm[:, 5:6], scalar2=None, op0=ALU.mult),
                                 reads=[otd, smd], writes=[ond])
                        else:
                            S.op('dve', lambda E, sm=sm, o0=o0, on=on, s=s: E.tensor_scalar(out=on[:], in0=o0[:, s, 0:128], scalar1=sm[:, 0:1], scalar2=None, op0=ALU.mult),
                                 reads=[o0d, smd], writes=[ond])
                        ons.append((on, ond))

                    def make_flush(ons=ons, h=h, off=off, tt=tt):
                        def f():
                            for s_, (on_, ond_) in enumerate(ons):
                                S.op('pe', lambda E: E.transpose(out=psTb[:, s_ * 128:(s_ + 1) * 128], in_=on_[:], identity=self.identb[:]),
                                     reads=[ond_, self.d_const], writes=[self.dps[6]])
                            ao, aod = aor.next()
                            S.op('act', lambda E: E.activation(out=ao[:, 0:tt], in_=psTb[:, 0:tt], func=AF.Copy), reads=[self.dps[6]], writes=[aod])
                            S.dma('sp', self.attnT[h, :, off:off + tt], ao[:, 0:tt], reads=[aod])
                        return f
                    pending.append(make_flush())
            while pending:
                pending.pop(0)()
        S.barrier()

    def stage_wo(self, l, kind):
        nc, S = self.nc, self.S
        j = l // 3
        lam_init = 0.8 - 0.6 * math.exp(-0.3 * l)
        wsrc = {'da': self.da_wo[j] if kind == 'da' else None, 'gq': self.gq_wo[0], 'rw': self.rw_wo[0]}[kind]
        with ExitStack() as es:
            wst = self.sb(es, "wost", [128, 8, D], F32)
            wo = self.sb(es, "wo", [128, 8, D], BF16)
            dw = Dep()
            S.dma('sp', wst[:], wsrc.rearrange("(k p) n -> p k n", p=128), writes=[dw])
            if kind == 'da':
                sg = self.sb(es, "sublng", [128, 1])
                S.dma('sp', sg[:], self.da_subln[j].rearrange("(p o) -> p o", o=1), writes=[dw])
                S.op('dve', lambda E: E.tensor_scalar(out=sg[:], in0=sg[:], scalar1=(1.0 - lam_init), scalar2=None, op0=ALU.mult), reads=[dw], writes=[dw])
                for k in range(8):
                    S.op('dve', lambda E, k=k: E.tensor_scalar(out=wo[:, k, :], in0=wst[:, k, :], scalar1=sg[:, 0:1], scalar2=None, op0=ALU.mult),
                         reads=[dw], writes=[dw])
            else:
                for k in range(8):
                    S.op('dve' if k % 2 else 'pool', lambda E, k=k: E.tensor_copy(out=wo[:, k, :], in_=wst[:, k, :]), reads=[dw], writes=[dw])
            atr = Ring(nc, es, "at", [128, 8, 512], BF16, 2)
            hres = Ring(nc, es, "whres", [128, 512], F32, 3)
            for (off, tt) in TILES:
                if off == 0 and l == DEPTH - 1:
                    continue
                w = 1 if off == 0 else 0
                at, atd = atr.next()
                S.dma('sp', at[:, :, 0:tt], self.attnT.rearrange("c p t -> p c t")[:, :, off:off + tt], writes=[atd])
                for c in range(8):
                    b = c % 4
                    hr, hrd = hres.next()
                    S.dma('sp', hr[:, 0:tt], self.hT[c, :, off:off + tt], writes=[hrd])
                    for k in range(8):
                        S.op('pe', lambda E, k=k, c=c, b=b, at=at: E.matmul(self.ps[b][:, 0:tt], lhsT=wo[:, k, c * 128:(c + 1) * 128], rhs=at[:, k, 0:tt],
                                                                     start=(k == 0), stop=(k == 7)), reads=[dw, atd], writes=[self.dps[b]])
                    S.op('dve', lambda E, hr=hr, c=c, b=b, w=w: E.scalar_tensor_tensor(
                        out=hr[:, 0:tt], in0=self.ps[b][:, 0:tt], scalar=self.mod(l, w, 5, c), in1=hr[:, 0:tt], op0=ALU.mult, op1=ALU.add),
                        reads=[self.dps[b], hrd, self.d_mod], writes=[hrd])
                    S.dma('pool', self.hT[c, :, off:off + tt], hr[:, 0:tt], reads=[hrd])
        S.barrier()

    def stage_rwkv(self, l):
        nc, S = self.nc, self.S
        RT = 256
        NCH = RT // 64
        NRT = T // RT
        GN_EPS = 64e-5
        uTs = nc.dram_tensor("rw_uT", [8, 128, T], F32).ap()
        o0T = nc.dram_tensor("rw_o0T", [8, 128, T], F32).ap()
        with ExitStack() as es:
            nb = self.make_norm_bufs(es)
            ur = Ring(nc, es, "rwu", [128, 8, 512], F32, 2)
            for (off, tt) in TILES:
                u, ud = ur.next()
                self.norm_mod(nb, off, tt, l, 3, u, ud)
                S.dma('pool', uTs.rearrange("c p t -> p c t")[:, :, off:off + tt], u[:, :, 0:tt], reads=[ud])
        S.barrier()
        with ExitStack() as es:
            class B:
                __slots__ = ('ap', 'd')

                def __init__(s, ap, d=None):
                    s.ap = ap
                    s.d = d if d is not None else Dep()

            def newb(name, shape, dt=F32):
                t = self.sb(es, name, shape, dt)
                return B(t[:], Dep()), t

            def TTo(eng, o, a, b, op):
                S.op(eng, lambda E: E.tensor_tensor(out=o.ap, in0=a.ap, in1=b.ap, op=op), reads=[a.d, b.d], writes=[o.d])

            def TS(eng, o, a, s1, s2, op0, op1=None, extra=()):
                if op1 is None:
                    S.op(eng, lambda E: E.tensor_scalar(out=o.ap, in0=a.ap, scalar1=s1, scalar2=None, op0=op0), reads=[a.d, *extra], writes=[o.d])
                else:
                    S.op(eng, lambda E: E.tensor_scalar(out=o.ap, in0=a.ap, scalar1=s1, scalar2=s2, op0=op0, op1=op1), reads=[a.d, *extra], writes=[o.d])

            def STT(o, a, sc, b, op0, op1, extra=()):
                S.op('dve', lambda E: E.scalar_tensor_tensor(out=o.ap, in0=a.ap, scalar=sc, in1=b.ap, op0=op0, op1=op1), reads=[a.d, b.d, *extra], writes=[o.d])

            def ACT(o, a, func, scale=1.0, bias=None, extra=()):
                if bias is None:
                    S.op('act', lambda E: E.activation(out=o.ap, in_=a.ap, func=func, scale=scale), reads=[a.d, *extra], writes=[o.d])
                else:
                    S.op('act', lambda E: E.activation(out=o.ap, in_=a.ap, func=func, scale=scale, bias=bias), reads=[a.d, *extra], writes=[o.d])

            def MM(o, lhsT, rhs, start, stop, reads):
                S.op('pe', lambda E: E.matmul(o.ap, lhsT=lhsT, rhs=rhs, start=start, stop=stop), reads=reads, writes=[o.d])

            PSB = [B(self.ps[i][:, 0:RT], self.dps[i]) for i in range(8)]
            dW = Dep()
            def loadw(name, src, shape, view):
                t = self.sb(es, name, shape, BF16)
                S.dma('pool', t[:], view, writes=[dW])
                return t
            wr = loadw("rwr", self.rw_wr, [128, 8, D], self.rw_wr[0].rearrange("(k p) n -> p k n", p=128))
            wk = loadw("rwk", self.rw_wk, [128, 8, D], self.rw_wk[0].rearrange("(k p) n -> p k n", p=128))
            wv = loadw("rwv", self.rw_wv, [128, 8, D], self.rw_wv[0].rearrange("(k p) n -> p k n", p=128))
            def allocw(name, shape):
                return self.sb(es, name, shape, BF16)
            w1s, a1s, g1s = allocw("rw1", [128, 8, 64]), allocw("ra1", [128, 8, 64]), allocw("rg1", [128, 8, 128])
            w2s, a2s, g2s = allocw("rw2", [64, D]), allocw("ra2", [64, D]), allocw("rg2", [128, D])
            w1 = [w1s, w1s]; a1 = [a1s, a1s]; g1 = [g1s, g1s]
            w2 = [w2s, w2s]; a2 = [a2s, a2s]; g2 = [g2s, g2s]

            def load_dir_weights(d):
                S.dma('pool', w1s[:], self.rw_w1[0, d].rearrange("(k p) n -> p k n", p=128), writes=[dW])
                S.dma('pool', a1s[:], self.rw_a1[0, d].rearrange("(k p) n -> p k n", p=128), writes=[dW])
                S.dma('pool', g1s[:], self.rw_g1[0, d].rearrange("(k p) n -> p k n", p=128), writes=[dW])
                S.dma('pool', w2s[:], self.rw_w2[0, d], writes=[dW])
                S.dma('pool', a2s[:], self.rw_a2[0, d], writes=[dW])
                S.dma('pool', g2s[:], self.rw_g2[0, d], writes=[dW])
            vec = self.sb(es, "rwvec", [128, 26, 8])
            cst = self.sb(es, "rwcst", [128, 5, RT])
            rmask = self.sb(es, "rwrm", [128, RT])
            blk = self.sb(es, "rwblk", [128, 128])
            blkb = self.sb(es, "rwblkb", [128, 128], BF16)
            coef0 = self.sb(es, "rwc0", [128, 6, 8])
            gneps = self.sb(es, "gneps", [128, 1])
            dV = Dep()
            S.dma('sp', vec[:], self.rw_vec, writes=[dV])
            S.dma('sp', cst[:], self.rw_cst, writes=[dV])
            S.dma('sp', rmask[:], self.rw_rmask, writes=[dV])
            lmask = self.sb(es, "rwlm", [128, 6, RT], BF16)
            S.dma('pool', lmask[:], self.rw_lmask, writes=[dV])
            S.dma('sp', blk[:], self.rw_blk, writes=[dV])
            S.op('dve', lambda E: E.tensor_copy(out=blkb[:], in_=blk[:]), reads=[dV], writes=[dV])
            S.op('dve', lambda E: E.memset(gneps[:], GN_EPS), writes=[dV])
            S.op('dve', lambda E: E.tensor_tensor(out=coef0[:], in0=vec[:, 0:6, :], in1=vec[:, 6:12, :], op=ALU.add), reads=[dV], writes=[dV])
            S.op('dve', lambda E: E.tensor_scalar(out=coef0[:], in0=coef0[:], scalar1=-1.0, scalar2=1.0, op0=ALU.mult, op1=ALU.add), reads=[dV], writes=[dV])
            zt, _ = newb("zt", [128, 8, RT + 2])
            ztt = _
            xm = [newb(f"xm{i}", [128, 8, RT], BF16) for i in range(6)]
            lw = [newb("lwT", [64, RT], BF16), newb("laT", [64, RT], BF16), newb("lgT", [128, RT], BF16)]
            tmpr = Ring(nc, es, "rwt", [128, RT], F32, 21)

            def tmp():
                t, d = tmpr.next()
                return B(t[:], d)
            tbr = Ring(nc, es, "rwtb", [128, RT], BF16, 12)

            def tmpb():
                t, d = tbr.next()
                return B(t[:], d)
            AT = [newb(f"AT{p}", [128, RT], BF16) for p in range(8)]
            RTl = [newb(f"RTl{p}", [128, RT], BF16) for p in range(8)]
            Aak = [newb(f"Aak{p}", [128, RT], BF16) for p in range(8)]
            Arb = [newb(f"Arb{p}", [128, RT], BF16) for p in range(8)]
            Ark = [newb(f"Ark{p}", [128, RT], BF16) for p in range(8)]
            Gm = [newb(f"Gm{p}", [128, RT]) for p in range(8)]
            Bst = [newb(f"Bst{p}", [128, RT], BF16) for p in range(8)]
            Kst = [newb(f"Kst{p}", [128, RT], BF16) for p in range(8)]
            PC = [newb(f"PC{p}", [128, NCH]) for p in range(8)]
            bon = [newb(f"bon{p}", [128, RT]) for p in range(8)]
            gg = [newb(f"gg{p}", [128, RT], BF16) for p in range(8)]
            Ysb, Yt = newb("Ysb", [128, 8, RT])
            Vst, Vstt = newb("Vst", [128, NCH, 512], BF16)
            Sf, Sft = newb("Sf", [128, 8, 64])
            Sb, Sbt = newb("Sbb", [128, 8, 64], BF16)
            Xsb, Xt = newb("Xsb", [128, 512])
            Nf = [newb("Nf0", [128, RT]), newb("Nf1", [128, RT])]
            NTf = [newb("NTf0", [128, RT]), newb("NTf1", [128, RT])]
            Usb, Ut = newb("Usb", [128, 512], BF16)
            o0r = Ring(nc, es, "rwo0", [128, RT], F32, 2)
            obr = Ring(nc, es, "rwob", [128, RT], BF16, 2)

            def blockmm32(o, lt, rt, rd):
                for ch in range(NCH):
                    for hd in range(2):
                        rs_ = slice(64 * hd, 64 * hd + 64)
                        cs_ = slice(ch * 64, (ch + 1) * 64)
                        S.op('pe', lambda E: E.matmul(o.ap[rs_, cs_], lhsT=lt[rs_, cs_], rhs=rt[rs_, cs_], start=True, stop=True), reads=rd, writes=[o.d])

            def tmp16():
                t_ = tmp()
                return B(t_.ap.bitcast(BF16)[:, 0:RT], t_.d)

            def chain_pairs(prs):
                II = B(cst[:, 4, :], dV)
                Jd, JTd = {}, {}
                for p_ in prs:
                    J, JT = tmp16(), tmp16()
                    nf, ntf = Nf[p_ % 2][0], NTf[p_ % 2][0]
                    S.op('dve', lambda E: E.tensor_tensor(out=J.ap, in0=nf.ap, in1=lmask[:, 0, :], op=ALU.mult), reads=[nf.d, dV], writes=[J.d])
                    TTo('pool', J, J, II, ALU.add)
                    S.op('dve', lambda E: E.tensor_tensor(out=JT.ap, in0=ntf.ap, in1=lmask[:, 0, :], op=ALU.mult), reads=[ntf.d, dV], writes=[JT.d])
                    TTo('pool', JT, JT, II, ALU.add)
                    Jd[p_], JTd[p_] = J, JT
                for lev in range(1, 6):
                    Nm, NTm, T1s, T1t = {}, {}, {}, {}
                    for p_ in prs:
                        pbp = [PSB[(4 * (p_ % 2)) + q] for q in range(4)]
                        nf, ntf = Nf[p_ % 2][0], NTf[p_ % 2][0]
                        Nm[p_], NTm[p_], T1s[p_] = tmp16(), tmp16(), tmp16()
                        S.op('pool', lambda E: E.tensor_tensor(out=Nm[p_].ap, in0=nf.ap, in1=lmask[:, lev, :], op=ALU.mult), reads=[nf.d, dV], writes=[Nm[p_].d])
                        S.op('dve', lambda E: E.tensor_tensor(out=NTm[p_].ap, in0=ntf.ap, in1=lmask[:, lev, :], op=ALU.mult), reads=[ntf.d, dV], writes=[NTm[p_].d])
                        blockmm32(pbp[3], NTm[p_].ap, Jd[p_].ap, [NTm[p_].d, Jd[p_].d])
                        ACT(T1s[p_], pbp[3], AF.Copy)
                    if lev < 5:
                        for p_ in prs:
                            pbp = [PSB[(4 * (p_ % 2)) + q] for q in range(4)]
                            T1t[p_] = tmp16()
                            blockmm32(pbp[1], Nm[p_].ap, JTd[p_].ap, [Nm[p_].d, JTd[p_].d])
                            S.op('dve', lambda E: E.tensor_copy(out=T1t[p_].ap, in_=pbp[1].ap), reads=[pbp[1].d], writes=[T1t[p_].d])
                    newJ, newJT = {}, {}
                    for p_ in prs:
                        pbp = [PSB[(4 * (p_ % 2)) + q] for q in range(4)]
                        blockmm32(pbp[0], JTd[p_].ap, T1s[p_].ap, [JTd[p_].d, T1s[p_].d])
                        Jn = tmp16() if lev < 5 else Gm[p_][0]
                        TTo('dve', Jn, pbp[0], Jd[p_], ALU.add)
                        newJ[p_] = Jn
                    if lev < 5:
                        for p_ in prs:
                            pbp = [PSB[(4 * (p_ % 2)) + q] for q in range(4)]
                            blockmm32(pbp[2], Jd[p_].ap, T1t[p_].ap, [Jd[p_].d, T1t[p_].d])
                            JTn = tmp16()
                            TTo('dve', JTn, pbp[2], JTd[p_], ALU.add)
                            newJT[p_] = JTn
                    for p_ in prs:
                        Jd[p_] = newJ[p_]
                        if lev < 5:
                            JTd[p_] = newJT[p_]

            def vcol(n, pr):
                return vec[:, n, pr:pr + 1]

            for d in range(2):
                load_dir_weights(d)
                S.op('dve', lambda E: E.memset(Sft[:], 0.0), reads=[Sf.d], writes=[Sf.d])
                S.op('dve', lambda E: E.memset(Sbt[:], 0.0), reads=[Sb.d], writes=[Sb.d])
                order = list(range(NRT)) if d == 0 else [0] + list(range(NRT - 1, 0, -1))
                mS, mI = (0, 1) if d == 0 else (2, 3)
                mST = 2 if d == 0 else 0
                for ti in order:
                    off = ti * RT
                    if ti == 0 and l == DEPTH - 1:
                        pass
                    seq_lo, seq_hi = (0, NCTX) if ti == 0 else (NCTX, T)
                    S.op('dve', lambda E: E.memset(ztt[:, :, 0:1], 0.0), reads=[zt.d], writes=[zt.d])
                    S.op('dve', lambda E: E.memset(ztt[:, :, RT + 1:RT + 2], 0.0), reads=[zt.d], writes=[zt.d])
                    lo = off - 1 if off > seq_lo else off
                    hi = off + RT + 1 if off + RT < seq_hi else off + RT
                    S.dma('sp', ztt[:, :, 1 - (off - lo): 1 + RT + (hi - off - RT)], uTs.rearrange("c p t -> p c t")[:, :, lo:hi], reads=[zt.d], writes=[zt.d])
                    for i in range(6):
                        for c in range(8):
                            t1 = tmp()
                            S.op('pool', lambda E: E.tensor_scalar(out=t1.ap, in0=ztt[:, c, 1:RT + 1], scalar1=coef0[:, i, c:c + 1], scalar2=None, op0=ALU.mult),
                                 reads=[zt.d, dV], writes=[t1.d])
                            S.op('dve', lambda E: E.scalar_tensor_tensor(out=t1.ap, in0=ztt[:, c, 0:RT], scalar=vec[:, i, c:c + 1], in1=t1.ap, op0=ALU.mult, op1=ALU.add),
                                 reads=[zt.d, dV, t1.d], writes=[t1.d])
                            S.op('dve', lambda E: E.scalar_tensor_tensor(out=xm[i][1][:, c, :], in0=ztt[:, c, 2:RT + 2], scalar=vec[:, 6 + i, c:c + 1], in1=t1.ap,
                                                                         op0=ALU.mult, op1=ALU.add), reads=[zt.d, dV, t1.d], writes=[xm[i][0].d])
                    for k in range(8):
                        MM(B(self.ps[0][0:64, 0:RT], self.dps[0]), w1[d][:, k, :], xm[1][1][:, k, :], k == 0, k == 7, [dW, xm[1][0].d])
                    ACT(lw[0][0], B(self.ps[0][0:64, 0:RT], self.dps[0]), AF.Tanh)
                    for k in range(8):
                        MM(B(self.ps[1][0:64, 0:RT], self.dps[1]), a1[d][:, k, :], xm[4][1][:, k, :], k == 0, k == 7, [dW, xm[4][0].d])
                    ACT(lw[1][0], B(self.ps[1][0:64, 0:RT], self.dps[1]), AF.Copy)
                    for k in range(8):
                        MM(PSB[2], g1[d][:, k, :], xm[5][1][:, k, :], k == 0, k == 7, [dW, xm[5][0].d])
                    ACT(lw[2][0], PSB[2], AF.Sigmoid)
                    wv4 = [wv[:, k, :].rearrange("p (pr hd v) -> p pr hd v", hd=2, v=64) for k in range(8)]
                    for ch in range(NCH):
                        for hd in range(2):
                            for k in range(8):
                                S.op('pe', lambda E: E.matmul(self.ps[3][64 * hd:64 * hd + 64, 0:512], lhsT=xm[3][1][:, k, ch * 64:(ch + 1) * 64],
                                                              rhs=wv4[k][:, :, hd, :], start=(k == 0), stop=(k == 7)),
                                     reads=[dW, xm[3][0].d], writes=[self.dps[3]])
                        S.op('act', lambda E: E.activation(out=Vstt[:, ch, :], in_=self.ps[3][:, 0:512], func=AF.Copy), reads=[self.dps[3]], writes=[Vst.d])
                    def stageA(pr):
                        cols = slice(pr * 128, (pr + 1) * 128)
                        pb = [PSB[(4 * (pr % 2)) + q] for q in range(4)]
                        for k in range(8):
                            MM(pb[0], wr[:, k, cols], xm[0][1][:, k, :], k == 0, k == 7, [dW, xm[0][0].d])
                        for k in range(8):
                            MM(pb[1], wk[:, k, cols], xm[2][1][:, k, :], k == 0, k == 7, [dW, xm[2][0].d])
                        for k in range(8):
                            MM(pb[2], wv[:, k, cols], xm[3][1][:, k, :], k == 0, k == 7, [dW, xm[3][0].d])
                        r_, k_, v_ = tmp(), tmp(), tmp()
                        ACT(r_, pb[0], AF.Copy)
                        ACT(k_, pb[1], AF.Copy)
                        ACT(v_, pb[2], AF.Copy)
                        MM(pb[3], w2[d][:, cols], lw[0][1][:], True, True, [dW, lw[0][0].d])
                        sig = tmp()
                        ACT(sig, pb[3], AF.Sigmoid, bias=vcol(12 + 6 * d + 0, pr), extra=[dV])
                        MM(pb[0], a2[d][:, cols], lw[1][1][:], True, True, [dW, lw[1][0].d])
                        asig = tmp()
                        ACT(asig, pb[0], AF.Sigmoid, bias=vcol(12 + 6 * d + 1, pr), extra=[dV])
                        MM(pb[1], g2[d][:, cols], lw[2][1][:], True, True, [dW, lw[2][0].d])
                        ACT(gg[pr][0], pb[1], AF.Copy)
                        logw = tmp()
                        TS('dve', logw, sig, -0.6065306597126334, None, ALU.mult)
                        kkr = tmp()
                        TS('dve', kkr, k_, vcol(12 + 6 * d + 2, pr), None, ALU.mult, extra=[dV])
                        sqb = tmpb()
                        ACT(sqb, kkr, AF.Square)
                        MM(pb[2], blkb[:], sqb.ap, True, True, [dV, sqb.d])
                        nrm = tmp()
                        ACT(nrm, pb[2], AF.Sqrt)
                        TS('dve', nrm, nrm, 1e-12, None, ALU.max)
                        S.op('dve', lambda E: E.reciprocal(out=nrm.ap, in_=nrm.ap), reads=[nrm.d], writes=[nrm.d])
                        kk = tmp()
                        TTo('dve', kk, kkr, nrm, ALU.mult)
                        kd = tmp()
                        TS('dve', kd, asig, -1.0, vcol(12 + 6 * d + 3, pr), ALU.add, ALU.mult, extra=[dV])
                        STT(kd, kd, 1.0, k_, ALU.add, ALU.mult)
                        rk = tmpb()
                        STT(rk, r_, vcol(24 + d, pr), kd, ALU.mult, ALU.mult, extra=[dV])
                        MM(pb[3], blkb[:], rk.ap, True, True, [dV, rk.d])
                        TTo('dve', bon[pr][0], pb[3], v_, ALU.mult)
                        bsc = tmp()
                        TTo('pool', bsc, kk, asig, ALU.mult)
                        L = tmp()
                        S.op('dve', lambda E: E.tensor_tensor_scan(out=L.ap, data0=rmask[:], data1=logw.ap, initial=0.0, op0=ALU.mult, op1=ALU.add),
                             reads=[logw.d, dV], writes=[L.d])
                        L3 = L.ap.rearrange("p (c t) -> p c t", t=64)
                        tot = L3[:, :, 63:64]
                        Li, Le = tmp(), tmp()
                        if d == 0:
                            S.op('pool', lambda E: E.tensor_copy(out=Li.ap, in_=L.ap), reads=[L.d], writes=[Li.d])
                        else:
                            S.op('dve', lambda E: E.tensor_tensor(out=Li.ap.rearrange("p (c t) -> p c t", t=64), in0=tot.broadcast_to([128, NCH, 64]), in1=L3, op=ALU.subtract),
                                 reads=[L.d], writes=[Li.d])
                            TTo('dve', Li, Li, logw, ALU.add)
                        TTo('pool', Le, Li, logw, ALU.subtract)
                        E1, E2, E3, E4 = tmp(), tmp(), tmp(), tmp()
                        ACT(E1, Le, AF.Exp)
                        ACT(E2, Li, AF.Exp)
                        ACT(E3, Li, AF.Exp, scale=-1.0)
                        D4 = tmp()
                        S.op('dve', lambda E: E.tensor_tensor(out=D4.ap.rearrange("p (c t) -> p c t", t=64), in0=tot.broadcast_to([128, NCH, 64]),
                                                              in1=Li.ap.rearrange("p (c t) -> p c t", t=64), op=ALU.subtract), reads=[L.d, Li.d], writes=[D4.d])
                        ACT(E4, D4, AF.Exp)
                        S.op('act', lambda E: E.activation(out=PC[pr][1][:], in_=L3[:, :, 63], func=AF.Exp), reads=[L.d], writes=[PC[pr][0].d])
                        STT(AT[pr][0], kk, -1.0, E1, ALU.mult, ALU.mult)
                        TTo('pool', RTl[pr][0], r_, E2, ALU.mult)
                        Bt, Kt, Bh, Kh = tmpb(), tmpb(), tmpb(), tmpb()
                        TTo('dve', Bt, bsc, E3, ALU.mult)
                        TTo('pool', Kt, kd, E3, ALU.mult)
                        TTo('dve', Bh, bsc, E4, ALU.mult)
                        TTo('pool', Kh, kd, E4, ALU.mult)

                        return (pb, Bt, Kt, Bh, Kh)

                    def stageM(pr, pb, Bt, Kt, Bh, Kh):
                        def blockmm(o, lt, rt, rd):
                            for ch in range(NCH):
                                for hd in range(2):
                                    rs_ = slice(64 * hd, 64 * hd + 64)
                                    cs_ = slice(ch * 64, (ch + 1) * 64)
                                    S.op('pe', lambda E: E.matmul(o.ap[rs_, cs_], lhsT=lt[rs_, cs_], rhs=rt[rs_, cs_], start=True, stop=True), reads=rd, writes=[o.d])
                        blockmm(pb[0], Bt.ap, AT[pr][0].ap, [Bt.d, AT[pr][0].d])
                        S.op('dve', lambda E: E.tensor_tensor(out=Nf[pr % 2][0].ap, in0=pb[0].ap, in1=cst[:, mS, :], op=ALU.mult), reads=[pb[0].d, dV], writes=[Nf[pr % 2][0].d])
                        blockmm(pb[1], AT[pr][0].ap, Bt.ap, [Bt.d, AT[pr][0].d])
                        S.op('dve', lambda E: E.tensor_tensor(out=NTf[pr % 2][0].ap, in0=pb[1].ap, in1=cst[:, mST, :], op=ALU.mult), reads=[pb[1].d, dV], writes=[NTf[pr % 2][0].d])
                        blockmm(pb[2], Kt.ap, AT[pr][0].ap, [Kt.d, AT[pr][0].d])
                        S.op('dve', lambda E: E.tensor_tensor(out=Aak[pr][0].ap, in0=pb[2].ap, in1=cst[:, mS, :], op=ALU.mult), reads=[pb[2].d, dV], writes=[Aak[pr][0].d])
                        blockmm(pb[3], Bt.ap, RTl[pr][0].ap, [Bt.d, RTl[pr][0].d])
                        S.op('dve', lambda E: E.tensor_tensor(out=Arb[pr][0].ap, in0=pb[3].ap, in1=cst[:, mI, :], op=ALU.mult), reads=[pb[3].d, dV], writes=[Arb[pr][0].d])
                        blockmm(pb[0], Kt.ap, RTl[pr][0].ap, [Kt.d, RTl[pr][0].d])
                        S.op('dve', lambda E: E.tensor_tensor(out=Ark[pr][0].ap, in0=pb[0].ap, in1=cst[:, mI, :], op=ALU.mult), reads=[pb[0].d, dV], writes=[Ark[pr][0].d])
                        for (src_, dst, bank) in ((Bh, Bst[pr][0], pb[1]), (Kh, Kst[pr][0], pb[2])):
                            for ch in range(NCH):
                                for hd in range(2):
                                    rs_ = slice(64 * hd, 64 * hd + 64)
                                    cs_ = slice(ch * 64, (ch + 1) * 64)
                                    S.op('pe', lambda E: E.matmul(bank.ap[rs_, cs_], lhsT=src_.ap[rs_, cs_], rhs=self.identb[rs_, rs_], start=True, stop=True),
                                         reads=[src_.d, self.d_const], writes=[bank.d])
                            ACT(dst, bank, AF.Copy)
                    prevA = stageA(0)
                    for pr in range(8):
                        nxtA = stageA(pr + 1) if pr + 1 < 8 else None
                        stageM(pr, *prevA)
                        if pr % 2 == 1:
                            chain_pairs([pr - 1, pr])
                        prevA = nxtA
                    chs = list(range(NCH)) if d == 0 else list(range(NCH - 1, -1, -1))
                    pX, pU, pY, pS = PSB[0], PSB[1], PSB[2], PSB[3]
                    pXf = B(self.ps[0][:, 0:512], self.dps[0])
                    pUf = B(self.ps[1][:, 0:512], self.dps[1])
                    pYf = B(self.ps[2][:, 0:512], self.dps[2])
                    pSf = B(self.ps[3][:, 0:512], self.dps[3])
                    for ch in chs:
                        cs_ = slice(ch * 64, (ch + 1) * 64)
                        for pr in range(8):
                            ps_ = slice(pr * 64, (pr + 1) * 64)
                            for hd in range(2):
                                rs_ = slice(64 * hd, 64 * hd + 64)
                                S.op('pe', lambda E: E.matmul(pXf.ap[rs_, ps_], lhsT=AT[pr][0].ap[rs_, cs_], rhs=Sbt[rs_, pr, :], start=True, stop=False),
                                     reads=[AT[pr][0].d, Sb.d], writes=[pXf.d])
                                S.op('pe', lambda E: E.matmul(pXf.ap[rs_, ps_], lhsT=Aak[pr][0].ap[rs_, cs_], rhs=Vstt[rs_, ch, ps_], start=False, stop=True),
                                     reads=[Aak[pr][0].d, Vst.d], writes=[pXf.d])
                        S.op('act', lambda E: E.activation(out=Xt[:], in_=pXf.ap, func=AF.Copy), reads=[pXf.d], writes=[Xsb.d])
                        for pr in range(8):
                            ps_ = slice(pr * 64, (pr + 1) * 64)
                            for hd in range(2):
                                rs_ = slice(64 * hd, 64 * hd + 64)
                                S.op('pe', lambda E: E.matmul(pUf.ap[rs_, ps_], lhsT=Gm[pr][0].ap[rs_, cs_], rhs=Xt[rs_, ps_], start=True, stop=True),
                                     reads=[Gm[pr][0].d, Xsb.d], writes=[pUf.d])
                        S.op('dve', lambda E: E.tensor_copy(out=Ut[:], in_=pUf.ap), reads=[pUf.d], writes=[Usb.d])
                        for pr in range(8):
                            ps_ = slice(pr * 64, (pr + 1) * 64)
                            for hd in range(2):
                                rs_ = slice(64 * hd, 64 * hd + 64)
                                S.op('pe', lambda E: E.matmul(pYf.ap[rs_, ps_], lhsT=Sbt[rs_, pr, :], rhs=RTl[pr][0].ap[rs_, cs_], start=True, stop=False),
                                     reads=[RTl[pr][0].d, Sb.d], writes=[pYf.d])
                                S.op('pe', lambda E: E.matmul(pYf.ap[rs_, ps_], lhsT=Ut[rs_, ps_], rhs=Arb[pr][0].ap[rs_, cs_], start=False, stop=False),
                                     reads=[Arb[pr][0].d, Usb.d], writes=[pYf.d])
                                S.op('pe', lambda E: E.matmul(pYf.ap[rs_, ps_], lhsT=Vstt[rs_, ch, ps_], rhs=Ark[pr][0].ap[rs_, cs_], start=False, stop=True),
                                     reads=[Ark[pr][0].d, Vst.d], writes=[pYf.d])
                        S.op('act', lambda E: E.activation(out=Yt[:, :, cs_], in_=pYf.ap.rearrange("p (a t) -> p a t", t=64), func=AF.Copy), reads=[pYf.d], writes=[Ysb.d])
                        for pr in range(8):
                            ps_ = slice(pr * 64, (pr + 1) * 64)
                            for hd in range(2):
                                rs_ = slice(64 * hd, 64 * hd + 64)
                                S.op('pe', lambda E: E.matmul(pSf.ap[rs_, ps_], lhsT=Bst[pr][0].ap[rs_, cs_], rhs=Ut[rs_, ps_], start=True, stop=False),
                                     reads=[Bst[pr][0].d, Usb.d], writes=[pSf.d])
                                S.op('pe', lambda E: E.matmul(pSf.ap[rs_, ps_], lhsT=Kst[pr][0].ap[rs_, cs_], rhs=Vstt[rs_, ch, ps_], start=False, stop=True),
                                     reads=[Kst[pr][0].d, Vst.d], writes=[pSf.d])
                        for pr in range(8):
                            ps_ = slice(pr * 64, (pr + 1) * 64)
                            S.op('dve', lambda E: E.scalar_tensor_tensor(out=Sft[:, pr, :], in0=Sft[:, pr, :], scalar=PC[pr][1][:, ch:ch + 1], in1=pSf.ap[:, ps_],
                                                                         op0=ALU.mult, op1=ALU.add), reads=[Sf.d, PC[pr][0].d, pSf.d], writes=[Sf.d])
                        S.op('dve', lambda E: E.tensor_copy(out=Sbt[:], in_=Sft[:]), reads=[Sf.d], writes=[Sb.d])
                    if ti == 0 and l == DEPTH - 1:
                        continue
                    for pr in range(8):
                        pb = [PSB[4 + q] for q in range(4)]
                        y = B(Yt[:, pr, :], Ysb.d)
                        ysq = tmp()
                        ACT(ysq, y, AF.Square)
                        MM(pb[0], blk[:], y.ap, True, True, [dV, y.d])
                        MM(pb[1], blk[:], ysq.ap, True, True, [dV, ysq.d])
                        m = tmp()
                        ACT(m, pb[0], AF.Copy, scale=1.0 / 64)
                        msq = tmp()
                        TTo('pool', msq, m, m, ALU.mult)
                        var = tmp()
                        STT(var, pb[1], 1.0 / 64, msq, ALU.mult, ALU.subtract)
                        ACT(var, var, AF.Sqrt, bias=gneps[:, 0:1], extra=[dV])
                        S.op('dve', lambda E: E.reciprocal(out=var.ap, in_=var.ap), reads=[var.d], writes=[var.d])
                        yc = tmp()
                        TTo('pool', yc, y, m, ALU.subtract)
                        TTo('dve', yc, yc, var, ALU.mult)
                        ACT(yc, yc, AF.Identity, scale=vcol(12 + 6 * d + 4, pr), bias=vcol(12 + 6 * d + 5, pr), extra=[dV])
                        TTo('pool', yc, yc, bon[pr][0], ALU.add)
                        if d == 0:
                            ob, obd = o0r.next()
                            o = B(ob[:], obd)
                            TTo('dve', o, yc, gg[pr][0], ALU.mult)
                            S.dma('sp', o0T[pr, :, off:off + RT], o.ap, reads=[o.d])
                        else:
                            ob, obd = o0r.next()
                            o = B(ob[:], obd)
                            S.dma('sp', o.ap, o0T[pr, :, off:off + RT], writes=[o.d])
                            TTo('dve', yc, yc, gg[pr][0], ALU.mult)
                            ob2, ob2d = obr.next()
                            o2 = B(ob2[:], ob2d)
                            TTo('dve', o2, yc, o, ALU.add)
                            S.dma('sp', self.attnT[pr, :, off:off + RT], o2.ap, reads=[o2.d])
        S.barrier()

    def build(self):
        S = self.S
        self.setup()
        g = self.es_global
        self.epsc = self.sb(g, "epsc", [128, 1])
        S.op('dve', lambda E: E.memset(self.epsc[:], EPS), writes=[self.d_const])
        for st in self.plan:
            if st == 'init':
                self.stage_init()
            elif st == 'ada':
                self.stage_ada()
            elif st[0] == 'ffn':
                self.stage_ffn(st[1], st[2])
            elif st == 'final':
                self.stage_final()
            elif st[0] == 'dump':
                o = self.nc.dram_tensor("dbg_h%d" % st[1], [8, 128, T], F32, kind="ExternalOutput").ap()
                S.dma('sp', o, self.hT)
                S.barrier()
            elif st[0] == 'mix':
                l = st[1]
                kind = ('da', 'gq', 'rw')[l % 3]
                if kind == 'rw':
                    self.stage_rwkv(l)
                else:
                    self.stage_attn_proj(l, kind)
                    self.stage_attn_core(l, kind)
                self.stage_wo(l, kind)
            else:
                raise ValueError(st)
        self.dbg_out = {}
        for name in getattr(self, 'debug_dump', []):
            t = getattr(self, name)
            o = self.nc.dram_tensor("dbg_" + name, list(t.shape), t.dtype, kind="ExternalOutput").ap()
            S.dma('sp', o, t)
            self.dbg_out[name] = o
        S.finish()
        self.es_global.close()
        return self.nc


FULL_PLAN = ['init', 'ada']
for _l in range(DEPTH):
    FULL_PLAN += [('ffn', _l, 0), ('mix', _l), ('ffn', _l, 1)]
FULL_PLAN += ['final']


def rope_tables():
    f32 = np.float32
    rows = SEQ // 64
    row = np.repeat(np.arange(rows), 64).astype(f32)
    col = np.tile(np.arange(64), rows).astype(f32)
    out = {}
    for name, hd in (("da", 64), ("gq", 128)):
        nf = hd // 4
        inv = np.power(f32(10000.0), -np.arange(nf, dtype=f32) / f32(nf)).astype(f32)
        ang = np.concatenate([row[:, None] * inv, col[:, None] * inv], axis=-1).astype(f32)
        cos = np.cos(ang).astype(f32).T
        sin = np.sin(ang).astype(f32).T
        C = np.concatenate([cos, cos], axis=0)
        Sg = np.concatenate([-sin, sin], axis=0)
        reps = 128 // hd
        out["ropeC_" + name] = np.ascontiguousarray(np.tile(C, (reps, 1)))
        out["ropeS_" + name] = np.ascontiguousarray(np.tile(Sg, (reps, 1)))
    return out


def make_in_maps(inputs, ncores=8):
    f32 = np.float32
    ident = np.eye(128, dtype=f32)
    c_ctx = np.asarray(inputs["c_ctx"], f32)
    shared = {
        "ada_w": np.ascontiguousarray(inputs["ada_w"], f32),
        "ada_bT": np.ascontiguousarray(np.asarray(inputs["ada_b"], f32).reshape(DEPTH, 72, 128).transpose(2, 0, 1)),
        "ffn_wg": np.ascontiguousarray(inputs["ffn_wg"], f32),
        "ffn_wu": np.ascontiguousarray(inputs["ffn_wu"], f32),
        "ffn_wd": np.ascontiguousarray(inputs["ffn_wd"], f32),
        "final_gT": np.ascontiguousarray(np.asarray(inputs["final_g"], f32).reshape(8, 128).T),
        "ident": ident,
        "da_wqkv": np.ascontiguousarray(inputs["da_wqkv"], f32),
        "da_wo": np.ascontiguousarray(inputs["da_wo"], f32),
        "da_lamB": np.ascontiguousarray(np.broadcast_to(np.asarray(inputs["da_lam"], f32).reshape(2, 1, 256), (2, 128, 256))),
        "da_subln": np.ascontiguousarray(inputs["da_subln"], f32),
        "gq_wqkv": np.ascontiguousarray(inputs["gq_wqkv"], f32),
        "gq_wo": np.ascontiguousarray(inputs["gq_wo"], f32),
        "rw_wo": np.ascontiguousarray(inputs["rw_wo"], f32),
    }
    qg = np.asarray(inputs["gq_qk_g"], f32)[0]
    sw = np.concatenate([np.arange(64, 128), np.arange(0, 64)])
    shared["gq_gT"] = np.ascontiguousarray(np.stack([qg[0], qg[0][sw], qg[1], qg[1][sw]], axis=1))
    shared.update(rope_tables())
    for nm in ("rw_wr", "rw_wk", "rw_wv", "rw_w1", "rw_w2", "rw_a1", "rw_a2", "rw_g1", "rw_g2"):
        shared[nm] = np.ascontiguousarray(inputs[nm], f32)

    def fm(v):
        return np.asarray(v, f32).reshape(8, 128).T
    vecs = [fm(inputs["rw_mu"][0, a, i]) for a in range(2) for i in range(6)]
    for dd in range(2):
        for nm in ("rw_w0", "rw_a0", "rw_kk", "rw_ka", "rw_ln_g", "rw_ln_b"):
            vecs.append(fm(inputs[nm][0, dd]))
    for dd in range(2):
        vecs.append(fm(np.asarray(inputs["rw_rk"], f32)[0, dd].reshape(-1)))
    shared["rw_vec"] = np.ascontiguousarray(np.stack(vecs, axis=1))
    s_ = (np.arange(128) % 64)[:, None]
    t_ = (np.arange(256) % 64)[None, :]
    shared["rw_cst"] = np.ascontiguousarray(np.stack([(s_ < t_), (s_ <= t_), (s_ > t_), (s_ >= t_), (s_ == t_)], axis=1).astype(f32))
    shared["rw_rmask"] = np.ascontiguousarray(np.broadcast_to(((np.arange(256) % 64) != 0).astype(f32)[None, :], (128, 256)))
    lm = []
    for m_ in (1, 2, 4, 8, 16, 32):
        lm.append(((s_ // (2 * m_)) == (t_ // (2 * m_))) & ((s_ // m_) != (t_ // m_)))
    shared["rw_lmask"] = np.ascontiguousarray(np.stack(lm, axis=1).astype(f32))
    pb_ = np.arange(128) // 64
    shared["rw_blk"] = np.ascontiguousarray((pb_[:, None] == pb_[None, :]).astype(f32))
    maps = []
    for b in range(ncores):
        cT = np.stack([np.asarray(inputs["c"][b], f32).reshape(8, 128).T, c_ctx.reshape(8, 128).T], axis=-1)
        m = dict(shared)
        m["x"] = np.ascontiguousarray(inputs["x"][b], f32)
        m["ctx"] = np.ascontiguousarray(inputs["ctx"][b], f32)
        m["cT"] = np.ascontiguousarray(cT)
        maps.append(m)
    return maps


def kernel(**inputs):
    mk = MK(FULL_PLAN)
    nc = mk.build()
    maps = make_in_maps(inputs)
    maps = [{k: v for k, v in m.items() if k in mk.inp} for m in maps]
    res = run_bass_kernel_spmd(nc, maps, core_ids=list(range(8)))
    return np.stack([res.results[b]["out"] for b in range(8)], axis=0)
```
